# Optimizing a Trainium2 kernel written in Bass

```python
import math
import jax, jax.numpy as jnp
from jax import lax
import numpy as np

D_MODEL = 1024
BATCH = 8
SEQ = 8192
DEPTH = 4

CTX_LEN = 256
GRID_W = 64
EPS = 1e-6
ROPE_BASE = 10000.0
Q_BLOCK = 128
N_MOD = 9
D_FF = 2816
DIFF_HEADS = D_MODEL // 256
DIFF_HD = 64
DIFF_VD = 2 * DIFF_HD
MLA_HEADS = D_MODEL // 128
MLA_NOPE = 64
MLA_ROPE = 32
MLA_V = 64
MLA_Q_RANK = 3 * D_MODEL // 8
MLA_KV_RANK = D_MODEL // 4
IN_SIZES = (DIFF_HEADS * 2 * DIFF_HD, DIFF_HEADS * 2 * DIFF_HD, DIFF_HEADS * DIFF_VD, MLA_Q_RANK, MLA_KV_RANK, MLA_ROPE)
IN_W = sum(IN_SIZES)
IN_SPLITS = tuple(sum(IN_SIZES[:i + 1]) for i in range(len(IN_SIZES) - 1))
D_MIX = DIFF_HEADS * DIFF_VD + MLA_HEADS * MLA_V
POOL_WINDOWS = (2, 4, 8, 16)
POOL_GROUPS = len(POOL_WINDOWS)
POOL_GC = D_MODEL // POOL_GROUPS

kernel_name = 'hybrid_diffattn_mla_pool_macaron_dit'


def _rms(x, g):
    x32 = x.astype(jnp.float32)
    y = x32 * lax.rsqrt(jnp.mean(x32 * x32, axis=-1, keepdims=True) + EPS)
    return y.astype(x.dtype) * g


def _modulate(x, g, mod, i):
    return _rms(x, g) * (1.0 + mod[:, :, 3 * i + 1]) + mod[:, :, 3 * i]


def _swiglu(h, wg, wu, wd):
    return (jax.nn.silu(h @ wg) * (h @ wu)) @ wd


def _ffn_half(x, g, mod, i, wg, wu, wd):
    return x + 0.5 * mod[:, :, 3 * i + 2] * _swiglu(_modulate(x, g, mod, i), wg, wu, wd)


def _rope_tables(rows, cols, dim):
    q = dim // 4
    freqs = ROPE_BASE ** (-jnp.arange(q, dtype=jnp.float32) / q)
    ang = jnp.stack([rows.astype(jnp.float32)[:, None] * freqs, cols.astype(jnp.float32)[:, None] * freqs], axis=1)
    return jnp.cos(ang), jnp.sin(ang)


def _rope(x, cs):
    cos, sin = cs
    q = x.shape[-1] // 4
    xr = x.reshape(*x.shape[:-1], 2, 2, q)
    x1, x2 = xr[..., 0, :], xr[..., 1, :]
    shape = (cos.shape[0],) + (1,) * (x.ndim - 3) + (2, q)
    cos = cos.reshape(shape).astype(x.dtype)
    sin = sin.reshape(shape).astype(x.dtype)
    return jnp.stack([x1 * cos - x2 * sin, x2 * cos + x1 * sin], axis=-2).reshape(x.shape)


def _sweep_queries(fn, q):
    B, S = q.shape[:2]
    nb = S // Q_BLOCK
    qb = jnp.moveaxis(q.reshape(B, nb, Q_BLOCK, *q.shape[2:]), 1, 0)
    out = lax.map(fn, qb)
    return jnp.moveaxis(out, 0, 1).reshape(B, S, *out.shape[3:])


def _diff_attend(q, k, v, lam):
    s = jnp.einsum('bqhgd,bkhgd->bhgqk', q, k).astype(jnp.float32) * (DIFF_HD ** -0.5)
    p = jax.nn.softmax(s, axis=-1)
    a = (p[:, :, 0] - lam * p[:, :, 1]).astype(v.dtype)
    return jnp.einsum('bhqk,bkhe->bqhe', a, v)


def _mla_attend(q, k_nope, k_rope, v):
    qn, qr = q[..., :MLA_NOPE], q[..., MLA_NOPE:]
    s = jnp.einsum('bqhd,bkhd->bhqk', qn, k_nope) + jnp.einsum('bqhd,bkd->bhqk', qr, k_rope)
    p = jax.nn.softmax(s.astype(jnp.float32) * ((MLA_NOPE + MLA_ROPE) ** -0.5), axis=-1)
    return jnp.einsum('bhqk,bkhd->bqhd', p.astype(v.dtype), v)


def _attn_project(h, aw, rope):
    w_in, qk_g, q_a_g, w_q_b, kv_a_g, w_kv_b, nope_g, rope_g = aw
    B, L, _ = h.shape
    dq, dk, dv, cq, ckv, kr = jnp.split(h @ w_in, IN_SPLITS, axis=-1)
    dq = _rms(dq.reshape(B, L, DIFF_HEADS, 2, DIFF_HD), qk_g[0])
    dk = _rms(dk.reshape(B, L, DIFF_HEADS, 2, DIFF_HD), qk_g[1])
    dv = dv.reshape(B, L, DIFF_HEADS, DIFF_VD)
    mq = (_rms(cq, q_a_g) @ w_q_b).reshape(B, L, MLA_HEADS, MLA_NOPE + MLA_ROPE)
    kv = (_rms(ckv, kv_a_g) @ w_kv_b).reshape(B, L, MLA_HEADS, MLA_NOPE + MLA_V)
    qn = _rms(mq[..., :MLA_NOPE], nope_g[0])
    qr = _rms(mq[..., MLA_NOPE:], rope_g[0])
    kn = _rms(kv[..., :MLA_NOPE], nope_g[1])
    mv = kv[..., MLA_NOPE:]
    kr = _rms(kr, rope_g[1])
    if rope is not None:
        rope_d, rope_m = rope
        dq, dk = _rope(dq, rope_d), _rope(dk, rope_d)
        qr, kr = _rope(qr, rope_m), _rope(kr, rope_m)
    return dq, dk, dv, jnp.concatenate([qn, qr], axis=-1), kn, kr, mv


def _merge(diff_o, mla_o, subln_g, lam_init, w_out):
    B, L = diff_o.shape[:2]
    d = _rms(diff_o, subln_g) * (1.0 - lam_init)
    return jnp.concatenate([d.reshape(B, L, -1), mla_o.reshape(B, L, -1)], axis=-1) @ w_out


def _attn_mixer(hl, hc, aw, lam, lam_init, subln_g, w_out, rope_l, with_ctx_queries):
    dq_l, dk_l, dv_l, mq_l, mk_l, mr_l, mv_l = _attn_project(hl, aw, rope_l)
    dq_c, dk_c, dv_c, mq_c, mk_c, mr_c, mv_c = _attn_project(hc, aw, None)
    dk = jnp.concatenate([dk_c, dk_l], axis=1)
    dv = jnp.concatenate([dv_c, dv_l], axis=1)
    mk = jnp.concatenate([mk_c, mk_l], axis=1)
    mr = jnp.concatenate([mr_c, mr_l], axis=1)
    mv = jnp.concatenate([mv_c, mv_l], axis=1)
    diff_l = _sweep_queries(lambda qb: _diff_attend(qb, dk, dv, lam), dq_l)
    mla_l = _sweep_queries(lambda qb: _mla_attend(qb, mk, mr, mv), mq_l)
    yl = _merge(diff_l, mla_l, subln_g, lam_init, w_out)
    yc = None
    if with_ctx_queries:
        diff_c = _diff_attend(dq_c, dk_c, dv_c, lam)
        mla_c = _mla_attend(mq_c, mk_c, mr_c, mv_c)
        yc = _merge(diff_c, mla_c, subln_g, lam_init, w_out)
    return yl, yc


def _pool_mixer(h, w_pool, scale):
    B, L, D = h.shape
    h32 = h.astype(jnp.float32)
    prefix = jnp.concatenate([jnp.zeros((B, 1, D), jnp.float32), jnp.cumsum(h32, axis=1)], axis=1)
    t = jnp.arange(L)
    outs = []
    for gi, w in enumerate(POOL_WINDOWS):
        sl = slice(gi * POOL_GC, (gi + 1) * POOL_GC)
        lo = jnp.clip(t - w // 2, 0, L)
        hi = jnp.clip(t + (w - w // 2), 0, L)
        cnt = (hi - lo).astype(jnp.float32)[None, :, None]
        pg = prefix[..., sl]
        outs.append((pg[:, hi] - pg[:, lo]) / cnt - h32[..., sl])
    d = jnp.stack(outs, axis=2).astype(h.dtype)
    y = jnp.einsum('blgc,gcd->blgd', d, w_pool).reshape(B, L, D)
    return y * scale


def setup_inputs(seed: int = 0) -> dict:
    key = jax.random.key(seed)
    ks = jax.random.split(key, 24)
    f32 = jnp.float32
    n_even = (DEPTH + 1) // 2
    n_odd = DEPTH // 2

    def nrm(k, shape, scale):
        return jax.random.normal(k, shape, f32) * scale

    def gain(k, shape):
        return 1.0 + 0.05 * jax.random.normal(k, shape, f32)

    return {
        'x': nrm(ks[0], (BATCH, SEQ, D_MODEL), 1.0),
        'c': nrm(ks[1], (BATCH, D_MODEL), 1.0),
        'ctx': nrm(ks[2], (BATCH, CTX_LEN, D_MODEL), 1.0),
        'c_ctx': nrm(ks[3], (D_MODEL,), 1.0),
        'mod_w': nrm(ks[4], (DEPTH, D_MODEL, N_MOD * D_MODEL), 0.5 * D_MODEL ** -0.5),
        'mod_b': nrm(ks[5], (DEPTH, N_MOD * D_MODEL), 0.01),
        'norm_g': gain(ks[6], (DEPTH, 3, D_MODEL)),
        'ffn_w_gate': nrm(ks[7], (DEPTH, 2, D_MODEL, D_FF), D_MODEL ** -0.5),
        'ffn_w_up': nrm(ks[8], (DEPTH, 2, D_MODEL, D_FF), D_MODEL ** -0.5),
        'ffn_w_down': nrm(ks[9], (DEPTH, 2, D_FF, D_MODEL), D_FF ** -0.5),
        'attn_w_in': nrm(ks[10], (n_even, D_MODEL, IN_W), D_MODEL ** -0.5),
        'diff_qk_g': gain(ks[11], (n_even, 2, DIFF_HD)),
        'diff_lambda': nrm(ks[12], (n_even, 4, DIFF_HD), 0.1),
        'diff_subln_g': gain(ks[13], (n_even, DIFF_VD)),
        'mla_q_a_g': gain(ks[14], (n_even, MLA_Q_RANK)),
        'mla_w_q_b': nrm(ks[15], (n_even, MLA_Q_RANK, MLA_HEADS * (MLA_NOPE + MLA_ROPE)), MLA_Q_RANK ** -0.5),
        'mla_kv_a_g': gain(ks[16], (n_even, MLA_KV_RANK)),
        'mla_w_kv_b': nrm(ks[17], (n_even, MLA_KV_RANK, MLA_HEADS * (MLA_NOPE + MLA_V)), MLA_KV_RANK ** -0.5),
        'mla_nope_g': gain(ks[18], (n_even, 2, MLA_NOPE)),
        'mla_rope_g': gain(ks[19], (n_even, 2, MLA_ROPE)),
        'attn_w_out': nrm(ks[20], (n_even, D_MIX, D_MODEL), D_MIX ** -0.5),
        'pool_w': nrm(ks[21], (n_odd, POOL_GROUPS, POOL_GC, POOL_GC), POOL_GC ** -0.5),
        'pool_scale': gain(ks[22], (n_odd, D_MODEL)),
    }


def reference(x, c, ctx, c_ctx, mod_w, mod_b, norm_g, ffn_w_gate, ffn_w_up, ffn_w_down,
              attn_w_in, diff_qk_g, diff_lambda, diff_subln_g, mla_q_a_g, mla_w_q_b,
              mla_kv_a_g, mla_w_kv_b, mla_nope_g, mla_rope_g, attn_w_out, pool_w, pool_scale):
    B, S, D = x.shape
    n_rows = S // GRID_W
    rows = jnp.repeat(jnp.arange(n_rows), GRID_W)
    cols = jnp.tile(jnp.arange(GRID_W), n_rows)
    rope_l = (_rope_tables(rows, cols, DIFF_HD), _rope_tables(rows, cols, MLA_ROPE))
    s_c = jax.nn.silu(c)
    s_cc = jax.nn.silu(c_ctx)[None]
    xl, xc = x, ctx
    for layer in range(DEPTH):
        even = layer % 2 == 0
        ctx_out = layer < DEPTH - 1
        ctx_in = ctx_out or even
        i = layer // 2
        g = norm_g[layer]
        fw1 = (ffn_w_gate[layer, 0], ffn_w_up[layer, 0], ffn_w_down[layer, 0])
        fw2 = (ffn_w_gate[layer, 1], ffn_w_up[layer, 1], ffn_w_down[layer, 1])
        mod_l = (s_c @ mod_w[layer] + mod_b[layer]).reshape(B, 1, N_MOD, D)
        xl = _ffn_half(xl, g[0], mod_l, 0, *fw1)
        hl = _modulate(xl, g[1], mod_l, 1)
        mod_c, hc = None, None
        if ctx_in:
            mod_c = (s_cc @ mod_w[layer] + mod_b[layer]).reshape(1, 1, N_MOD, D)
            xc = _ffn_half(xc, g[0], mod_c, 0, *fw1)
            hc = _modulate(xc, g[1], mod_c, 1)
        if even:
            lam_init = 0.8 - 0.6 * math.exp(-0.3 * layer)
            dl = diff_lambda[i].astype(jnp.float32)
            lam = jnp.exp(jnp.sum(dl[0] * dl[1])) - jnp.exp(jnp.sum(dl[2] * dl[3])) + lam_init
            aw = (attn_w_in[i], diff_qk_g[i], mla_q_a_g[i], mla_w_q_b[i], mla_kv_a_g[i],
                  mla_w_kv_b[i], mla_nope_g[i], mla_rope_g[i])
            yl, yc = _attn_mixer(hl, hc, aw, lam, lam_init, diff_subln_g[i], attn_w_out[i], rope_l, ctx_out)
        else:
            yl = _pool_mixer(hl, pool_w[i], pool_scale[i])
            yc = _pool_mixer(hc, pool_w[i], pool_scale[i]) if ctx_out else None
        xl = xl + mod_l[:, :, 5] * yl
        xl = _ffn_half(xl, g[2], mod_l, 2, *fw2)
        if ctx_out:
            xc = xc + mod_c[:, :, 5] * yc
            xc = _ffn_half(xc, g[2], mod_c, 2, *fw2)
    return xl
```

```python
import math
from contextlib import ExitStack

import numpy as np
import concourse.bass as bass
import concourse.mybir as mybir
from concourse.bass_utils import run_bass_kernel_spmd

F32 = mybir.dt.float32
BF16 = mybir.dt.bfloat16
AF = mybir.ActivationFunctionType
ALU = mybir.AluOpType

D = 1024
DFF = 2816
NF = DFF // 128
TB = 256
CTX = 256
GRID_W = 64
EPS = 1e-6
IN_W = 2208
N_CORES = 8

ENGS = ("pe", "act", "dve", "pool", "sp")


class _Op:
    __slots__ = ("eng", "fn", "deps", "dma_waits", "signal", "sigval", "is_dma", "key", "ep")

    def __init__(self, eng, fn, is_dma, key):
        self.ep = 0
        self.eng = eng
        self.fn = fn
        self.deps = []
        self.dma_waits = []
        self.signal = False
        self.sigval = 0
        self.is_dma = is_dma
        self.key = key


class Prog:
    def __init__(self, nc, stack):
        self.nc = nc
        self.st = stack
        self.ops = {e: [] for e in ENGS}
        self.last_w = {}
        self.readers = {}
        self.dma_cnt = {}
        self.nops = 0
        self.epoch = 0

    def _add_dep(self, op, d):
        if d is None or d is op:
            return
        if d.is_dma:
            op.dma_waits.append((d.key, self.dma_cnt[d.key] * 16))
        else:
            if d.eng == "pe" and op.eng == "pe":
                return
            d.signal = True
            op.deps.append(d)

    def op(self, eng, fn, reads=(), writes=(), dma=None):
        o = _Op(eng, fn, dma is not None, dma)
        o.ep = self.epoch if eng == "pe" else 0
        for r in reads:
            self._add_dep(o, self.last_w.get(r))
        for w in writes:
            self._add_dep(o, self.last_w.get(w))
            for rd in list(self.readers.get(w, {}).values()):
                self._add_dep(o, rd)
        rk = ("D", dma) if dma is not None else eng
        for r in reads:
            self.readers.setdefault(r, {})[rk] = o
        for w in writes:
            self.last_w[w] = o
            self.readers[w] = {}
        if dma is not None:
            self.dma_cnt[dma] = self.dma_cnt.get(dma, 0) + 1
        self.ops[eng].append(o)
        self.nops += 1
        return o

    def barrier(self):
        lasts = []
        for e in ENGS:
            for o in reversed(self.ops[e]):
                if not o.is_dma and o.fn is not None:
                    lasts.append(o)
                    break
        dmas = [(k, c * 16) for k, c in self.dma_cnt.items()]
        for e in ENGS:
            f = _Op(e, None, False, None)
            for d in lasts:
                if d.eng != e:
                    d.signal = True
                    f.deps.append(d)
            f.dma_waits = list(dmas)
            self.ops[e].append(f)
        self.last_w = {}
        self.readers = {}
        self.epoch += 1

    def emit(self):
        nc = self.nc
        sems = {}
        for e in ENGS:
            for ep in range(self.epoch + 1 if e == "pe" else 1):
                sems[("E", e, ep)] = self.st.enter_context(nc.semaphore("s_%s%d" % (e, ep)))
        for i, k in enumerate(self.dma_cnt):
            sems[("D", k)] = self.st.enter_context(nc.semaphore("d%d" % i))
        for e in ENGS:
            c = {}
            for o in self.ops[e]:
                if o.signal:
                    c[o.ep] = c.get(o.ep, 0) + 1
                    o.sigval = c[o.ep]
        block = self.st.enter_context(nc.Block())

        def run(e, eng):
            waited = {}
            for o in self.ops[e]:
                need = {}
                for d in o.deps:
                    s = ("E", d.eng, d.ep)
                    if need.get(s, 0) < d.sigval:
                        need[s] = d.sigval
                for k, v in o.dma_waits:
                    s = ("D", k)
                    if need.get(s, 0) < v:
                        need[s] = v
                for s, v in need.items():
                    if waited.get(s, 0) < v:
                        eng.wait_ge(sems[s], v)
                        waited[s] = v
                if o.fn is None:
                    continue
                ins = o.fn(eng)
                if o.is_dma:
                    ins.then_inc(sems[("D", o.key)], 16)
                elif o.signal:
                    ins.then_inc(sems[("E", e, o.ep)], 1)

        @block.tensor
        def _(eng):
            run("pe", eng)

        @block.scalar
        def _(eng):
            run("act", eng)

        @block.vector
        def _(eng):
            run("dve", eng)

        @block.gpsimd
        def _(eng):
            run("pool", eng)

        @block.sync
        def _(eng):
            run("sp", eng)


def lam_init_of(layer):
    return 0.8 - 0.6 * math.exp(-0.3 * layer)


class Builder:
    def __init__(self, NLB, layers, stop=None):
        self.layers = list(layers)
        n_layers = 4
        self.NLB = NLB
        self.NB = 1 + NLB
        self.T = self.NB * TB
        self.NKT = 2 * self.NB
        self.n_layers = n_layers
        self.stop = stop
        self.nc = bass.Bass("TRN2", target_bir_lowering=False)
        self.root = ExitStack()
        self.P = Prog(self.nc, self.root)
        self.uid = 0

    def din(self, name, shape, dt=F32):
        return self.nc.dram_tensor(name, list(shape), dt, kind="ExternalInput").ap()

    def dscr(self, name, shape, dt=F32):
        return self.nc.dram_tensor(name, list(shape), dt).ap()

    def sb(self, st, name, shape, dt=F32):
        self.uid += 1
        return st.enter_context(self.nc.sbuf_tensor("%s_%d" % (name, self.uid), list(shape), dt))

    def ps(self, st, name, shape=(128, 512), dt=F32):
        self.uid += 1
        return st.enter_context(self.nc.psum_tensor("%s_%d" % (name, self.uid), list(shape), dt))

    def dma(self, eng, out, in_, reads, writes, key):
        self.P.op(eng, lambda e: e.dma_start(out=out, in_=in_), reads=reads, writes=writes, dma=key)

    def mm(self, out, lhsT, rhs, start, stop, reads, writes):
        self.P.op("pe", lambda e: e.matmul(out, lhsT, rhs, start=start, stop=stop), reads=reads, writes=writes)

    def declare(self):
        NB, NLB, T = self.NB, self.NLB, self.T
        self.xin = self.din("xin", [NB, 128, 8, TB])
        self.cT = self.din("cT", [128, 8, 2])
        self.mod_w = self.din("mod_w", [4, D, 9 * D])
        self.mod_bT = self.din("mod_bT", [128, 4, 72])
        self.gT = self.din("gT", [128, 4, 3, 8])
        self.pscl = self.din("pscl", [128, 2, 8])
        self.wg = self.din("ffn_w_gate", [4, 2, D, DFF])
        self.wu = self.din("ffn_w_up", [4, 2, D, DFF])
        self.wd = self.din("ffn_w_down", [4, 2, DFF, D])
        self.pool_w = self.din("pool_w", [2, 4, 256, 256])
        self.invc = self.din("invc", [NB, 4, TB])
        self.w_in = self.din("attn_w_in", [2, D, IN_W])
        self.w_qb = self.din("w_qb", [2, 384, 768])
        self.w_kvb = self.din("w_kvb", [2, 256, 1024])
        self.w_out = self.din("attn_w_out", [2, D, D])
        self.acols = self.din("acols", [128, 2, 16])
        self.lamv = self.din("lamv", [2, 1, 256])
        self.ropet = self.din("ropet", [4, 128, T])
        self.cmat = self.din("cmat", [128, 9, 128])
        self.out = self.nc.dram_tensor("out", [NLB, 128, 8, TB], F32, kind="ExternalOutput").ap()
        self.xs = [self.dscr("xs0", [NB, 128, 8, TB]), self.dscr("xs1", [NB, 128, 8, TB])]
        self.dqT = self.dscr("dqT", [4, 128, T], BF16)
        self.dkT = self.dscr("dkT", [4, 128, T], BF16)
        self.dV = self.dscr("dV", [4, 128, self.NKT, 128], BF16)
        self.mqT = self.dscr("mqT", [8, 96, T], BF16)
        self.mknT = self.dscr("mknT", [8, 64, T], BF16)
        self.krT = self.dscr("krT", [32, T], BF16)
        self.mV = self.dscr("mV", [4, 128, self.NKT, 2, 65], BF16)
        self.mT = self.dscr("mT", [8, 128, T], BF16)

    def build(self):
        self.declare()
        st = self.root
        self.modc = self.sb(st, "modc", [128, 4, 2, 9, 8])
        self.cm = self.sb(st, "cm", [128, 9, 128], BF16)
        self.onesf = self.sb(st, "onesf", [128, 128])
        self.acl = self.sb(st, "acl", [128, 2, 16])
        self.nlam = self.sb(st, "nlam", [128, 2])
        self.epsc = self.sb(st, "epsc", [128, 1])
        w1st = ExitStack()
        W1 = self.alloc_ffn_w(w1st)
        self.prologue(prefetch=lambda: self.issue_ffn_w(W1, self.layers[0], 0))
        self.P.barrier()
        src = self.xin
        cur = 0
        allb = list(range(self.NB))
        latb = list(range(1, self.NB))
        layers = self.layers
        for li, l in enumerate(layers):
            last_layer = li == len(layers) - 1
            even = l % 2 == 0
            bl = allb if l < 3 else latb
            bq = allb if l < 2 else latb
            dst = self.xs[cur]
            self.ffn_pass(l, 0, bl, src, dst, None, W=(W1 if li == 0 else None))
            self.P.barrier()
            if li == 0:
                w1st.close()
            src = dst
            cur ^= 1
            if self.stop == "ffn1":
                break
            if even:
                self.qkv_pass(l, bl, src)
                self.P.barrier()
                if self.stop == "qkv":
                    break
                self.attn_pass(l, with_ctx=(l < 2))
                self.P.barrier()
                if self.stop == "attn":
                    break
                with ExitStack() as wst:
                    W2 = self.alloc_ffn_w(wst)
                    dst = self.xs[cur]
                    self.merge_pass(l, bq, src, dst, prefetch=lambda W2=W2, l=l: self.issue_ffn_w(W2, l, 1))
                    self.P.barrier()
                    src = dst
                    cur ^= 1
                    if self.stop == "mix":
                        break
                    final = last_layer and self.stop is None
                    dst = self.out if final else self.xs[cur]
                    self.ffn_pass(l, 1, latb if final else bq, src, dst, final, W=W2)
                    self.P.barrier()
                    src = dst
                    cur ^= 1
            else:
                with ExitStack() as wst:
                    W2 = self.alloc_ffn_w(wst)
                    dst = self.xs[cur]
                    self.poolffn_pass(l, bq, src, dst, False, prefetch=lambda W2=W2, l=l: self.issue_ffn_w(W2, l, 1))
                    self.P.barrier()
                    src = dst
                    cur ^= 1
                    if self.stop == "mix":
                        break
                    final = last_layer and self.stop is None
                    dst = self.out if final else self.xs[cur]
                    self.ffn_pass(l, 1, latb if final else bq, src, dst, final, W=W2)
                    self.P.barrier()
                    src = dst
                    cur ^= 1
        if self.stop is not None:
            with ExitStack() as st2:
                t = self.sb(st2, "cp", [128, 8, TB])
                for b in latb:
                    self.dma("sp", t[:, :, :], src[b], [], ["cp"], "cp")
                    self.dma("sp", self.out[b - 1], t[:, :, :], ["cp"], [], "cp")
                self.P.barrier()
        self.P.emit()
        return self.nc

    def prologue(self, prefetch=None):
        P = self.P
        with ExitStack() as st:
            cT = self.sb(st, "cTt", [128, 8, 2])
            sT = self.sb(st, "sTt", [128, 8, 2])
            mb = self.sb(st, "mbt", [128, 4, 72])
            gT = self.sb(st, "gTt", [128, 4, 3, 8])
            psc = self.sb(st, "psct", [128, 2, 8])
            cmf = self.sb(st, "cmf", [128, 9, 128])
            mw = [self.sb(st, "mw%d" % i, [128, 8, 768]) for i in range(2)]
            mraw = self.sb(st, "mraw", [128, 72, 2])
            mo = [self.ps(st, "mo%d" % i) for i in range(2)]
            self.dma("sp", cT[:, :, :], self.cT, [], ["cT"], "c0")
            self.dma("sp", mb[:, :, :], self.mod_bT, [], ["mb"], "c1")
            self.dma("sp", gT[:, :, :, :], self.gT, [], ["gT"], "c2")
            self.dma("sp", psc[:, :, :], self.pscl, [], ["psc"], "c3")
            self.dma("sp", cmf[:, :, :], self.cmat, [], ["cmf"], "c4")
            self.dma("sp", self.acl[:, :, :], self.acols, [], ["acl"], "c5")
            if prefetch is not None:
                prefetch()
            P.op("dve", lambda e: e.tensor_copy(self.cm[:, :, :], cmf[:, :, :]), ["cmf"], ["cm"])
            P.op("dve", lambda e: e.memset(self.epsc[:, :], EPS), [], ["epsc"])
            P.op("act", lambda e: e.activation(sT[:, :, :], cT[:, :, :], AF.Silu), ["cT"], ["sT"])
            P.op("pool", lambda e: e.memset(self.onesf[:, :], 1.0), [], ["onesf"])
            lt = self.sb(st, "lt", [1, 2, 256])
            pr = self.sb(st, "pr", [1, 2, 2, 64])
            sm = self.sb(st, "sm", [1, 8])
            self.dma("sp", lt[:, :, :], self.lamv.rearrange("i o n -> o i n"), [], ["lt"], "c6")
            for i in range(2):
                li_ = lam_init_of(2 * i)
                for h in range(2):
                    a0 = lt[0:1, i, (2 * h) * 64:(2 * h + 1) * 64]
                    a1 = lt[0:1, i, (2 * h + 1) * 64:(2 * h + 2) * 64]
                    o = pr[0:1, i, h, :]
                    P.op("dve", lambda e, o=o, a0=a0, a1=a1: e.tensor_tensor(o, a0, a1, ALU.mult), ["lt"], ["pr"])
                P.op("dve", lambda e, i=i: e.reduce_sum(sm[0:1, 4 * i:4 * i + 2], pr[0:1, i, :, :], mybir.AxisListType.X),
                     ["pr"], ["sm"])
                P.op("act", lambda e, i=i: e.activation(sm[0:1, 4 * i:4 * i + 2], sm[0:1, 4 * i:4 * i + 2], AF.Exp), ["sm"], ["sm"])
                P.op("dve", lambda e, i=i: e.tensor_tensor(sm[0:1, 4 * i + 2:4 * i + 3], sm[0:1, 4 * i + 1:4 * i + 2],
                                                          sm[0:1, 4 * i:4 * i + 1], ALU.subtract), ["sm"], ["sm"])
                P.op("dve", lambda e, i=i, li_=li_: e.tensor_scalar(sm[0:1, 4 * i + 3:4 * i + 4], sm[0:1, 4 * i + 2:4 * i + 3],
                                                                   -li_, None, ALU.add), ["sm"], ["sm"])
                self.mm(mo[0][:, 100 + i:101 + i], self.onesf[0:1, :], sm[0:1, 4 * i + 3:4 * i + 4], True, True,
                        ["onesf", "sm"], ["mo0"])
                P.op("dve", lambda e, i=i: e.tensor_copy(self.nlam[:, i:i + 1], mo[0][:, 100 + i:101 + i]), ["mo0"], ["nlam"])
                P.op("dve", lambda e, i=i, li_=li_: e.tensor_scalar(self.acl[:, i, 12:13], self.acl[:, i, 11:12], 1.0 - li_, None, ALU.mult),
                     ["acl"], ["acl"])
            for l in self.layers:
                for q in range(12):
                    buf = mw[q % 2]
                    bk = "mw%d" % (q % 2)
                    src = self.mod_w[l, :, q * 768:(q + 1) * 768].rearrange("(k p) n -> p k n", p=128)
                    for k2 in range(2):
                        self.dma("sp" if k2 == 0 else "act", buf[:, k2 * 4:(k2 + 1) * 4, :], src[:, k2 * 4:(k2 + 1) * 4, :], [], [bk], bk)
                    pt = mo[q % 2]
                    pk = "mo%d" % (q % 2)
                    for jj in range(6):
                        for k in range(8):
                            self.mm(pt[:, jj * 2:jj * 2 + 2], buf[:, k, jj * 128:(jj + 1) * 128], sT[:, k, :],
                                    k == 0, k == 7, [bk, "sT"], [pk])
                    o = mraw[:, q * 6:(q + 1) * 6, :]
                    i0 = pt[:, 0:12].rearrange("p (a b) -> p a b", b=2)
                    i1 = mb[:, l, q * 6:(q + 1) * 6].unsqueeze(2).to_broadcast([128, 6, 2])
                    P.op("dve", lambda e, o=o, i0=i0, i1=i1: e.tensor_tensor(o, i0, i1, ALU.add), [pk, "mb"], ["mraw"])
                for s in range(2):
                    for i in range(3):
                        sc = mraw[:, (3 * i + 1) * 8:(3 * i + 2) * 8, s]
                        sh = mraw[:, (3 * i) * 8:(3 * i + 1) * 8, s]
                        ga = mraw[:, (3 * i + 2) * 8:(3 * i + 3) * 8, s]
                        A = self.modc[:, l, s, 3 * i + 0, :]
                        Bc = self.modc[:, l, s, 3 * i + 1, :]
                        G = self.modc[:, l, s, 3 * i + 2, :]
                        g = gT[:, l, i, :]
                        P.op("dve", lambda e, A=A, sc=sc, g=g: e.scalar_tensor_tensor(A, sc, 1.0, g, ALU.add, ALU.mult),
                             ["mraw", "gT"], ["modc"])
                        P.op("dve", lambda e, Bc=Bc, sh=sh: e.tensor_copy(Bc, sh), ["mraw"], ["modc"])
                        if i == 1 and l % 2 == 1:
                            pcl = psc[:, l // 2, :]
                            P.op("dve", lambda e, G=G, ga=ga, pcl=pcl: e.tensor_tensor(G, ga, pcl, ALU.mult),
                                 ["mraw", "psc"], ["modc"])
                        else:
                            f = 1.0 if i == 1 else 0.5
                            P.op("dve", lambda e, G=G, ga=ga, f=f: e.tensor_scalar(G, ga, f, None, ALU.mult),
                                 ["mraw"], ["modc"])

    def norm_mod(self, tl, xt, xk, n, l, s, i, out, outk, fp32_out=False):
        P = self.P
        sq, rstd, tmp, ssp = tl["sq"], tl["rstd"], tl["ntmp"], tl["ss"]
        P.op("pool", lambda e: e.tensor_tensor(sq[:, :, :n], xt, xt, ALU.mult), [xk], ["sq"])
        for j in range(8):
            self.mm(ssp[:, :n], self.cm[:, 0, :], sq[:, j, :n], j == 0, j == 7, ["sq", "cm"], ["ss"])
        P.op("act", lambda e: e.activation(rstd[:, :n], ssp[:, :n], AF.Ln, bias=self.epsc[:, 0:1], scale=1.0),
             ["ss", "epsc"], ["rstd"])
        P.op("act", lambda e: e.activation(rstd[:, :n], rstd[:, :n], AF.Exp, scale=-0.5), ["rstd"], ["rstd"])
        rb = rstd[:, :n].unsqueeze(1).to_broadcast([128, 8, n])
        P.op("dve", lambda e: e.tensor_tensor(tmp[:, :, :n], xt, rb, ALU.mult), [xk, "rstd"], ["ntmp"])
        for j in range(8):
            A = self.modc[:, l, s, 3 * i, j:j + 1]
            Bc = self.modc[:, l, s, 3 * i + 1, j:j + 1]
            o = out[:, j, :n]
            ti = tmp[:, j, :n]
            eng = "dve" if (j % 2 == 0) else "pool"
            P.op(eng, lambda e, o=o, ti=ti, A=A, Bc=Bc: e.tensor_scalar(o, ti, A, Bc, ALU.mult, ALU.add),
                 ["ntmp", "modc"], [outk])

    def alloc_ffn_w(self, st):
        wg = self.sb(st, "wg", [128, 8, DFF], BF16)
        wu = self.sb(st, "wu", [128, 8, DFF], BF16)
        wd = self.sb(st, "wd", [128, NF, D], BF16)
        return wg, wu, wd

    def issue_ffn_w(self, W, l, w):
        wg, wu, wd = W
        sg_ = self.wg[l, w].rearrange("(k p) n -> p k n", p=128)
        su_ = self.wu[l, w].rearrange("(k p) n -> p k n", p=128)
        sd_ = self.wd[l, w].rearrange("(f p) n -> p f n", p=128)
        H = 11 * 128
        for (c0, c1, sfx) in ((0, H, "A"), (H, DFF, "B")):
            for k in range(8):
                self.dma("pool", wg[:, k, c0:c1], sg_[:, k, c0:c1], [], ["wg" + sfx], "wg" + sfx)
                self.dma("pool", wu[:, k, c0:c1], su_[:, k, c0:c1], [], ["wu" + sfx], "wu" + sfx)
        for f in range(0, NF, 2):
            self.dma("pool", wd[:, f:f + 2, :], sd_[:, f:f + 2, :], [], ["wd"], "wd")

    def load_ffn_w(self, st, l, w):
        W = self.alloc_ffn_w(st)
        self.issue_ffn_w(W, l, w)
        return W

    def ffn_tiles(self, st):
        tl = {}
        tl["sq"] = self.sb(st, "sq", [128, 8, TB + 16], BF16)
        tl["rstd"] = self.sb(st, "rstd", [128, TB + 16])
        tl["ntmp"] = self.sb(st, "ntmp", [128, 8, TB + 16])
        tl["ht"] = self.sb(st, "ht", [128, 8, TB], BF16)
        tl["acth"] = self.sb(st, "acth", [128, NF, TB], BF16)
        tl["sg"] = [self.sb(st, "sg%d" % i, [128, TB]) for i in range(2)]
        tl["xo"] = [self.sb(st, "xo%d" % i, [128, 8, TB]) for i in range(2)]
        tl["ss"] = self.ps(st, "ss")
        tl["gate"] = [self.ps(st, "gate%d" % i) for i in range(2)]
        tl["up"] = [self.ps(st, "up%d" % i) for i in range(2)]
        tl["y"] = [self.ps(st, "y%d" % i) for i in range(2)]
        return tl

    def ffn_B(self, tl, W):
        P = self.P
        wg, wu, wd = W
        ht, acth = tl["ht"], tl["acth"]
        for f in range(NF):
            gp, up_ = tl["gate"][f % 2], tl["up"][f % 2]
            gk, uk = "gate%d" % (f % 2), "up%d" % (f % 2)
            sfx = "A" if f < 11 else "B"
            for k in range(8):
                self.mm(gp[:, :TB], wg[:, k, f * 128:(f + 1) * 128], ht[:, k, :], k == 0, k == 7, ["wg" + sfx, "ht"], [gk])
            for k in range(8):
                self.mm(up_[:, :TB], wu[:, k, f * 128:(f + 1) * 128], ht[:, k, :], k == 0, k == 7, ["wu" + sfx, "ht"], [uk])
            sg = tl["sg"][f % 2]
            sk = "sg%d" % (f % 2)
            P.op("act", lambda e, sg=sg, gp=gp: e.activation(sg[:, :], gp[:, :TB], AF.Silu), [gk], [sk])
            o = acth[:, f, :]
            P.op("dve", lambda e, o=o, sg=sg, up_=up_: e.tensor_tensor(o, sg[:, :], up_[:, :TB], ALU.mult),
                 [sk, uk], ["acth"])

    def ffn_C(self, tl, W, xt, xk, l, s, i, it, dst_ap):
        P = self.P
        wg, wu, wd = W
        acth = tl["acth"]
        xo = tl["xo"][it % 2]
        xok = "xo%d" % (it % 2)
        for d in range(8):
            yp = tl["y"][d % 2]
            yk = "y%d" % (d % 2)
            for f in range(NF):
                self.mm(yp[:, :TB], wd[:, f, d * 128:(d + 1) * 128], acth[:, f, :], f == 0, f == NF - 1,
                        ["wd", "acth"], [yk])
            G = self.modc[:, l, s, 3 * i + 2, d:d + 1]
            o = xo[:, d, :]
            xi = xt[:, d, :]
            P.op("dve", lambda e, o=o, yp=yp, G=G, xi=xi: e.scalar_tensor_tensor(o, yp[:, :TB], G, xi, ALU.mult, ALU.add),
                 [yk, "modc", xk], [xok])
        self.dma("sp", dst_ap, xo[:, :, :], [xok], [], xok)

    def ffn_pass(self, l, w, blocks, src, dst, to_out, W=None):
        with ExitStack() as st:
            if W is None:
                W = self.load_ffn_w(st, l, w)
            tl = self.ffn_tiles(st)
            xts = [self.sb(st, "xt%d" % i, [128, 8, TB]) for i in range(2)]
            i = 0 if w == 0 else 2
            nb = len(blocks)

            def load(it):
                self.dma("sp", xts[it % 2][:, :, :], src[blocks[it]], [], ["xt%d" % (it % 2)], "xt%d" % (it % 2))

            def A(it):
                s = 1 if blocks[it] == 0 else 0
                self.norm_mod(tl, xts[it % 2][:, :, :], "xt%d" % (it % 2), TB, l, s, i, tl["ht"], "ht")

            load(0)
            if nb > 1:
                load(1)
            A(0)
            for it, b in enumerate(blocks):
                s = 1 if b == 0 else 0
                self.ffn_B(tl, W)
                if it + 1 < nb:
                    A(it + 1)
                self.ffn_C(tl, W, xts[it % 2][:, :, :], "xt%d" % (it % 2), l, s, i, it, dst[b - 1] if to_out else dst[b])
                if it + 2 < nb:
                    load(it + 2)

    def poolffn_pass(self, l, blocks, src, dst, to_out, prefetch=None):
        P = self.P
        NB = self.NB
        with ExitStack() as st:
            tl = {}
            tl["sq"] = self.sb(st, "sq", [128, 8, TB + 16], BF16)
            tl["rstd"] = self.sb(st, "rstd", [128, TB + 16])
            tl["ntmp"] = self.sb(st, "ntmp", [128, 8, TB + 16])
            tl["ss"] = self.ps(st, "ss")
            pw = self.sb(st, "pw", [128, 4, 2, 256], BF16)
            self.dma("pool", pw[:, :, :, :], self.pool_w[l // 2].rearrange("g (k p) n -> p g k n", p=128), [], ["pw"], "pw")
            if prefetch is not None:
                prefetch()
            xhs = [self.sb(st, "xh%d" % i, [128, 8, TB + 16]) for i in range(2)]
            hh = self.sb(st, "hh", [128, 8, TB + 16])
            ta = self.sb(st, "ta", [128, 2, TB + 16])
            tb_ = self.sb(st, "tb", [128, 2, TB + 16])
            ivc = self.sb(st, "ivc", [128, 4, TB])
            dT = self.sb(st, "dT", [128, 8, TB], BF16)
            xms = [self.sb(st, "xm%d" % i, [128, 8, TB]) for i in range(1)] * 2
            yp = self.ps(st, "yp")
            n = TB + 16
            def load(it):
                b = blocks[it]
                xh = xhs[it % 2]
                xk = "xh%d" % (it % 2)
                has_l = (b >= 2)
                has_r = (b >= 1 and b + 1 < NB)
                if not has_l:
                    P.op("pool", lambda e, xh=xh: e.memset(xh[:, :, 0:8], 0.0), [], [xk])
                if not has_r:
                    P.op("pool", lambda e, xh=xh: e.memset(xh[:, :, TB + 8:TB + 16], 0.0), [], [xk])
                self.dma("sp", xh[:, :, 8:TB + 8], src[b], [], [xk], xk)
                if has_l:
                    self.dma("sp", xh[:, :, 0:8], src[b - 1][:, :, TB - 8:TB], [], [xk], xk)
                if has_r:
                    self.dma("sp", xh[:, :, TB + 8:TB + 16], src[b + 1][:, :, 0:8], [], [xk], xk)

            load(0)
            for it, b in enumerate(blocks):
                xh = xhs[it % 2]
                xk = "xh%d" % (it % 2)
                s = 1 if b == 0 else 0
                has_l = (b >= 2)
                has_r = (b >= 1 and b + 1 < NB)
                if it + 1 < len(blocks):
                    load(it + 1)
                xm, xmk = xms[0], "xm0"
                self.dma("sp", ivc[:, :, :], self.invc[b].partition_broadcast(128), [], ["ivc"], "ivc")
                self.norm_mod(tl, xh[:, :, :], xk, n, l, s, 1, hh, "hh")
                if not has_l:
                    P.op("dve", lambda e: e.memset(hh[:, :, 0:8], 0.0), [], ["hh"])
                if not has_r:
                    P.op("dve", lambda e: e.memset(hh[:, :, TB + 8:TB + 16], 0.0), [], ["hh"])
                for gi in range(4):
                    c0 = 2 * gi
                    cur = hh[:, c0:c0 + 2, :]
                    curk = "hh"
                    ln = n
                    bufs = [(ta, "ta"), (tb_, "tb")]
                    for lev in range(gi + 1):
                        step = 1 << lev
                        nb_, nk = bufs[lev % 2]
                        ln2 = ln - step
                        a0 = cur[:, :, 0:ln2]
                        a1 = cur[:, :, step:step + ln2]
                        o = nb_[:, :, 0:ln2]
                        P.op("dve", lambda e, o=o, a0=a0, a1=a1: e.tensor_tensor(o, a0, a1, ALU.add), [curk], [nk])
                        cur, curk, ln = nb_, nk, ln2
                    half = 1 << gi
                    S = cur[:, :, 8 - half:8 - half + TB]
                    iv = ivc[:, gi, :].unsqueeze(1).to_broadcast([128, 2, TB])
                    ob, ok = bufs[(gi + 1) % 2]
                    o1 = ob[:, :, 0:TB]
                    P.op("dve", lambda e, o1=o1, S=S, iv=iv: e.tensor_tensor(o1, S, iv, ALU.mult), [curk, "ivc"], [ok])
                    hm = hh[:, c0:c0 + 2, 8:TB + 8]
                    od = dT[:, c0:c0 + 2, :]
                    P.op("dve", lambda e, od=od, o1=o1, hm=hm: e.tensor_tensor(od, o1, hm, ALU.subtract), [ok, "hh"], ["dT"])
                    for oc in range(2):
                        for k in range(2):
                            self.mm(yp[:, :TB], pw[:, gi, k, oc * 128:(oc + 1) * 128], dT[:, c0 + k, :], k == 0, k == 1,
                                    ["pw", "dT"], ["yp"])
                        G = self.modc[:, l, s, 5, c0 + oc:c0 + oc + 1]
                        o = xm[:, c0 + oc, :]
                        xi = xh[:, c0 + oc, 8:TB + 8]
                        P.op("dve", lambda e, o=o, G=G, xi=xi: e.scalar_tensor_tensor(o, yp[:, :TB], G, xi, ALU.mult, ALU.add),
                             ["yp", "modc", xk], [xmk])
                self.dma("sp", dst[b], xm[:, :, :], [xmk], [], xmk)


    def qkv_pass(self, l, blocks, src):
        P = self.P
        i = l // 2
        T = self.T
        with ExitStack() as st:
            win = self.sb(st, "win", [128, 8, IN_W], BF16)
            wqb = self.sb(st, "wqb", [128, 3, 768], BF16)
            wkvb = self.sb(st, "wkvb", [128, 2, 1024], BF16)
            s_in = self.w_in[i].rearrange("(k p) n -> p k n", p=128)
            for k in range(8):
                self.dma("pool", win[:, k, :], s_in[:, k, :], [], ["win"], "win")
            self.dma("pool", wqb[:, :, :], self.w_qb[i].rearrange("(k p) n -> p k n", p=128), [], ["wqb"], "wqb")
            self.dma("pool", wkvb[:, :, :], self.w_kvb[i].rearrange("(k p) n -> p k n", p=128), [], ["wkvb"], "wkvb")
            tl = {}
            tl["sq"] = self.sb(st, "sq", [128, 8, TB], BF16)
            tl["rstd"] = self.sb(st, "rstd", [128, TB])
            tl["ntmp"] = self.sb(st, "ntmp", [128, 8, TB])
            tl["ss"] = self.ps(st, "ss")
            ht = self.sb(st, "ht", [128, 8, TB], BF16)
            xts = [self.sb(st, "xt%d" % k, [128, 8, TB]) for k in range(2)]
            rtbs = [self.sb(st, "rtb%d" % k, [128, 4, TB]) for k in range(2)]
            raws = [self.ps(st, "raw%d" % k) for k in range(3)]
            gsss = [self.ps(st, "gss%d" % k) for k in range(2)]
            gss = gsss[0]
            prm = self.ps(st, "prm")
            vts = [self.ps(st, "vt%d" % k) for k in range(1)]
            sqgs = [self.sb(st, "sqg%d" % k, [128, TB], BF16) for k in range(2)]
            rss = [self.sb(st, "rs%d" % k, [128, TB]) for k in range(2)]
            rs = rss[0]
            qns = [self.sb(st, "qn%d" % k, [128, TB], BF16) for k in range(2)]
            t1 = self.sb(st, "t1", [128, TB])
            t2 = self.sb(st, "t2", [128, TB])
            stq = [self.sb(st, "stq%d" % k, [128, TB], BF16) for k in range(3)]
            vst = [self.sb(st, "vst%d" % k, [128, 4, 128], BF16) for k in range(2)]
            mvst = [self.sb(st, "mvst%d" % k, [128, 8, 65], BF16) for k in range(2)]
            craw = self.sb(st, "craw", [128, 3, TB])
            csq = self.sb(st, "csq", [128, 3, TB], BF16)
            cn = self.sb(st, "cn", [128, 3, TB], BF16)
            for k in range(2):
                P.op("pool", lambda e, k=k: e.memset(mvst[k][:, :, 64:65], 1.0), [], ["mvst%d" % k])
            cnt = {"raw": 0, "qn": 0, "stq": 0, "vt": 0}

            def load(it):
                b = blocks[it]
                self.dma("sp", xts[it % 2][:, :, :], src[b], [], ["xt%d" % (it % 2)], "xt%d" % (it % 2))
                self.dma("sp", rtbs[it % 2][:, :, :], self.ropet[:, :, b * TB:(b + 1) * TB].rearrange("a p t -> p a t"),
                         [], ["rtb%d" % (it % 2)], "rtb%d" % (it % 2))

            def proj(W, wk, nk, col0, ncol, rhs_t, rhs_k):
                r = cnt["raw"] % 3
                cnt["raw"] += 1
                pt, pk = raws[r], "raw%d" % r
                for k in range(nk):
                    self.mm(pt[:ncol, :TB], W[:, k, col0:col0 + ncol], rhs_t[:, k, :], k == 0, k == nk - 1, [wk, rhs_k], [pk])
                return pt, pk

            def group_norm_a(pt, pk, np_, midx):
                r = cnt["qn"] % 2
                cnt["qn"] += 1
                sq_, sqk = sqgs[r], "sqg%d" % r
                gs_, gsk = gsss[r], "gss%d" % r
                P.op("act", lambda e: e.activation(sq_[:np_, :], pt[:np_, :TB], AF.Square), [pk], [sqk])
                self.mm(gs_[:np_, :TB], self.cm[:np_, midx, :np_], sq_[:np_, :], True, True, ["cm", sqk], [gsk])
                return r

            def group_norm_b(r, pt, pk, np_, gcol):
                qn, qk = qns[r], "qn%d" % r
                rs_, rsk = rss[r], "rs%d" % r
                gs_, gsk = gsss[r], "gss%d" % r
                P.op("act", lambda e: e.activation(rs_[:np_, :], gs_[:np_, :TB], AF.Ln, bias=self.epsc[:np_, 0:1], scale=1.0),
                     [gsk, "epsc"], [rsk])
                P.op("act", lambda e: e.activation(rs_[:np_, :], rs_[:np_, :], AF.Exp, scale=-0.5), [rsk], [rsk])
                P.op("dve", lambda e: e.scalar_tensor_tensor(qn[:np_, :], pt[:np_, :TB], gcol, rs_[:np_, :], ALU.mult, ALU.mult),
                     [pk, rsk, "acl"], [qk])
                return qn, qk

            def rope(qn, qk, np_, pidx, rtb, rk, ci, si, do_rope):
                if not do_rope:
                    return qn, qk
                r = cnt["stq"] % 3
                cnt["stq"] += 1
                o, ok = stq[r], "stq%d" % r
                self.mm(prm[:np_, :TB], self.cm[:np_, pidx, :np_], qn[:np_, :], True, True, ["cm", qk], ["prm"])
                P.op("pool", lambda e: e.tensor_tensor(t1[:np_, :], qn[:np_, :], rtb[:np_, ci, :], ALU.mult), [qk, rk], ["t1"])
                P.op("dve", lambda e: e.tensor_tensor(t2[:np_, :], prm[:np_, :TB], rtb[:np_, si, :], ALU.mult), ["prm", rk], ["t2"])
                P.op("pool", lambda e: e.tensor_tensor(o[:np_, :], t1[:np_, :], t2[:np_, :], ALU.add), ["t1", "t2"], [ok])
                return o, ok

            def gn_job(W, wk, nk, col0, ncol, rhs_t, rhs_k, midx, gcol, rope_args, stores):
                stt = {}

                def s0():
                    stt["pt"], stt["pk"] = proj(W, wk, nk, col0, ncol, rhs_t, rhs_k)

                def s1():
                    stt["r"] = group_norm_a(stt["pt"], stt["pk"], ncol, midx)

                def s1b():
                    stt["qn"], stt["qk"] = group_norm_b(stt["r"], stt["pt"], stt["pk"], ncol, gcol)

                def s2():
                    if rope_args is not None:
                        o, ok = rope(stt["qn"], stt["qk"], ncol, *rope_args)
                    else:
                        o, ok = stt["qn"], stt["qk"]
                    for (dap, p0, p1) in stores:
                        self.dma("sp", dap, o[p0:p1, :], [ok], [], ok)
                return [s0, s1, s1b, s2], False

            def vt_job(lhs_t, lhs_k, nk, tt, W, wk, c0, dst_tile, dst_key, dsl, dram_ap, sb_ap, d_):
                stt = {}

                def s0():
                    vp, vk = vts[0], "vt0"
                    for k in range(nk):
                        self.mm(vp[:, :512], lhs_t[:, k, tt * 128:(tt + 1) * 128], W[:, k, c0:c0 + 512], k == 0, k == nk - 1, [lhs_k, wk], [vk])
                    P.op("act", lambda e: e.copy(dsl, vp[:, :512].rearrange("p (h d) -> p h d", d=d_)), [vk], [dst_key])

                def s1():
                    self.dma("sp", dram_ap, sb_ap, [dst_key], [], dst_key)
                return [s0, s1], False

            def craw_job(c0, c):
                stt = {}

                def s0():
                    stt["pt"], stt["pk"] = proj(win, "win", 8, c0 + c * 128, 128, ht, "ht")

                def s1():
                    pt = stt["pt"]
                    P.op("act", lambda e: e.copy(craw[:, c, :], pt[:, :TB]), [stt["pk"]], ["craw"])
                    P.op("pool", lambda e: e.tensor_tensor(csq[:, c, :], craw[:, c, :], craw[:, c, :], ALU.mult), ["craw"], ["csq"])
                return [s0, s1], False

            def combine_job(nch, midx, g0):
                def s0():
                    for c in range(nch):
                        self.mm(gss[:, :TB], self.cm[:, midx, :], csq[:, c, :], c == 0, c == nch - 1, ["cm", "csq"], ["gss0"])
                    P.op("act", lambda e: e.activation(rs[:, :], gss[:, :TB], AF.Ln, bias=self.epsc[:, 0:1], scale=1.0),
                         ["gss0", "epsc"], ["rs0"])
                    P.op("act", lambda e: e.activation(rs[:, :], rs[:, :], AF.Exp, scale=-0.5), ["rs0"], ["rs0"])
                    for c in range(nch):
                        gc = self.acl[:, i, g0 + c:g0 + c + 1]
                        P.op("dve", lambda e, c=c, gc=gc: e.scalar_tensor_tensor(cn[:, c, :], craw[:, c, :], gc, rs[:, :], ALU.mult, ALU.mult),
                             ["craw", "rs0", "acl"], ["cn"])
                return [s0], True

            def run_jobs(jobs):
                n = len(jobs)
                emitted = [0] * n
                for t in range(n + 4):
                    if t < n and jobs[t][1]:
                        for j in range(t):
                            while emitted[j] < len(jobs[j][0]):
                                jobs[j][0][emitted[j]]()
                                emitted[j] += 1
                    if t < n:
                        jobs[t][0][0]()
                        emitted[t] = 1
                    for s_ in (1, 2, 3):
                        j = t - s_
                        if 0 <= j < n and emitted[j] == s_ and s_ < len(jobs[j][0]):
                            jobs[j][0][s_]()
                            emitted[j] += 1
                for j in range(n):
                    assert emitted[j] == len(jobs[j][0]), (j, emitted[j])

            load(0)
            for it, b in enumerate(blocks):
                xt = xts[it % 2]
                xk = "xt%d" % (it % 2)
                rtb = rtbs[it % 2]
                rk = "rtb%d" % (it % 2)
                if it + 1 < len(blocks):
                    load(it + 1)
                s = 1 if b == 0 else 0
                lat = b != 0
                cols = slice(b * TB, (b + 1) * TB)
                self.norm_mod(tl, xt[:, :, :], xk, TB, l, s, 1, ht, "ht")
                jobs = []
                ra_d = (6, rtb, rk, 0, 1, True) if lat else None
                ra_m = (7, rtb, rk, 2, 3, True) if lat else None
                for c in range(8):
                    dstT = self.dqT if c < 4 else self.dkT
                    gcol = self.acl[:, i, (0 if c < 4 else 1):(1 if c < 4 else 2)]
                    jobs.append(gn_job(win, "win", 8, c * 128, 128, ht, "ht", 4, gcol, ra_d, [(dstT[c % 4][:, cols], 0, 128)]))
                for tt in range(2):
                    vs, vsk = vst[tt], "vst%d" % tt
                    jobs.append(vt_job(ht, "ht", 8, tt, win, "win", 1024, vs, vsk, vs[:, :, :],
                                       self.dV[:, :, 2 * b + tt, :].rearrange("h p d -> p h d"), vs[:, :, :], 128))
                for c in range(3):
                    jobs.append(craw_job(1536, c))
                jobs.append(combine_job(3, 1, 6))
                for n_ in range(4):
                    jobs.append(gn_job(wqb, "wqb", 3, n_ * 128, 128, cn, "cn", 4, self.acl[:, i, 2:3], None,
                                       [(self.mqT[2 * n_ + hh][0:64, cols], hh * 64, (hh + 1) * 64) for hh in range(2)]))
                for r2 in range(2):
                    jobs.append(gn_job(wqb, "wqb", 3, 512 + r2 * 128, 128, cn, "cn", 5, self.acl[:, i, 4:5], ra_m,
                                       [(self.mqT[4 * r2 + j][64:96, cols], j * 32, (j + 1) * 32) for j in range(4)]))
                for c in range(2):
                    jobs.append(craw_job(1920, c))
                jobs.append(combine_job(2, 2, 9))
                for n_ in range(4):
                    jobs.append(gn_job(wkvb, "wkvb", 2, n_ * 128, 128, cn, "cn", 4, self.acl[:, i, 3:4], None,
                                       [(self.mknT[2 * n_ + hh][:, cols], hh * 64, (hh + 1) * 64) for hh in range(2)]))
                for tt in range(2):
                    ms, msk = mvst[tt], "mvst%d" % tt
                    jobs.append(vt_job(cn, "cn", 2, tt, wkvb, "wkvb", 512, ms, msk, ms[:, :, 0:64],
                                       self.mV[:, :, 2 * b + tt, :, :].rearrange("u p h e -> p u h e"),
                                       ms[:, :, :].rearrange("p (u h) e -> p u h e", h=2), 64))
                jobs.append(gn_job(win, "win", 8, 2176, 32, ht, "ht", 5, self.acl[0:32, i, 5:6], ra_m,
                                   [(self.krT[:, cols], 0, 32)]))
                run_jobs(jobs)

    def attn_pass(self, l, with_ctx):
        P = self.P
        i = l // 2
        T, NKT, NLB = self.T, self.NKT, self.NLB
        QB = 512
        with ExitStack() as st:
            kU = [self.sb(st, "kU%d" % k, [128, 2, T], BF16) for k in range(2)]
            vU = [self.sb(st, "vU%d" % k, [128, NKT, 130], BF16) for k in range(2)]
            qU = [self.sb(st, "qU%d" % k, [128, 2, QB], BF16) for k in range(2)]
            pT = [self.sb(st, "pT%d" % k, [128, 2, QB], BF16) for k in range(3)]
            sp_ = [self.ps(st, "sps%d" % k, [128, 2, 512]) for k in range(3)]
            O = [self.ps(st, "o%d" % k) for k in range(2)]
            L = [sp_[2][:, 0, :], sp_[2][:, 1, :]]
            rg = self.sb(st, "rg", [128, QB])
            og = [self.sb(st, "og%d" % k, [128, QB]) for k in range(2)]
            dd = self.sb(st, "dd", [128, QB])
            sqd = self.sb(st, "sqd", [128, QB], BF16)
            rsd = self.sb(st, "rsd", [128, QB])
            rbs = self.sb(st, "rbs", [128, QB])
            stg = [self.sb(st, "stg%d" % k, [128, QB], BF16) for k in range(4)]
            rrow = self.sb(st, "rrow", [128, QB])
            accs = [self.sb(st, "acc%d" % k, [128, 2, QB]) for k in range(2)]
            SPL = 768
            sel = self.sb(st, "sel", [128, 64])
            P.op("pool", lambda e: e.memset(rrow[:, :], 0.0), [], ["rrow"])
            P.op("pool", lambda e: e.memset(sel[:, :], 0.0), [], ["sel"])
            P.op("pool", lambda e: e.memset(sel[64:65, :], 1.0), [], ["sel"])
            qbs = []
            if with_ctx:
                qbs.append((0, TB, 2))
            for q in range(NLB // 2):
                qbs.append((TB + q * QB, QB, NKT))
            jobs = [(u, qb) for u in range(8) for qb in range(len(qbs))]

            def load_unit(u):
                k_, v_ = kU[u % 2], vU[u % 2]
                kk, vk = "kU%d" % (u % 2), "vU%d" % (u % 2)
                if u < 4:
                    self.dma("sp", k_[:, 0, :], self.dkT[u], [], [kk], kk)
                    self.dma("sp", v_[:, :, 0:128], self.dV[u], [], [vk], vk)
                else:
                    for h in range(2):
                        self.dma("sp", k_[0:64, h, :], self.mknT[2 * (u - 4) + h], [], [kk], kk)
                        self.dma("sp", k_[64:96, h, :], self.krT, [], [kk], kk)
                    self.dma("sp", v_[:, :, :], self.mV[u - 4].rearrange("p k h e -> p k (h e)"), [], [vk], vk)

            def load_q(j):
                u, qb = jobs[j]
                c0, nq, _ = qbs[qb]
                q_, qk = qU[j % 2], "qU%d" % (j % 2)
                if u < 4:
                    self.dma("sp", q_[:, 0, :nq], self.dqT[u][:, c0:c0 + nq], [], [qk], qk)
                else:
                    for h in range(2):
                        self.dma("sp", q_[0:96, h, :nq], self.mqT[2 * (u - 4) + h][:, c0:c0 + nq], [], [qk], qk)

            load_unit(0)
            load_q(0)
            scale_d = 1.0 / math.sqrt(64.0)
            scale_m = 1.0 / math.sqrt(96.0)
            it = 0
            nst = 0
            for j, (u, qb) in enumerate(jobs):
                c0, nq, nkt = qbs[qb]
                if qb == 0 and u + 1 < 8:
                    load_unit(u + 1)
                if j + 1 < len(jobs):
                    load_q(j + 1)
                k_, v_ = kU[u % 2], vU[u % 2]
                kk, vk = "kU%d" % (u % 2), "vU%d" % (u % 2)
                q_, qk = qU[j % 2], "qU%d" % (j % 2)
                diff = u < 4
                sc = scale_d if diff else scale_m

                def S_exp(kt, it0=it, k_=k_, q_=q_, kk=kk, qk=qk, nq=nq, diff=diff, sc=sc):
                    s_, sk = sp_[(it0 + kt) % 3], "sps%d" % ((it0 + kt) % 3)
                    p_, pk = pT[(it0 + kt) % 3], "pT%d" % ((it0 + kt) % 3)
                    kc = slice(kt * 128, (kt + 1) * 128)
                    for g in range(2):
                        if diff:
                            self.mm(s_[:, g, :nq], k_[g * 64:(g + 1) * 64, 0, kc], q_[g * 64:(g + 1) * 64, 0, :nq], True, True, [kk, qk], [sk])
                        else:
                            self.mm(s_[:, g, :nq], k_[0:96, g, kc], q_[0:96, g, :nq], True, True, [kk, qk], [sk])
                    P.op("act", lambda e: e.activation(p_[:, :, :nq], s_[:, :, :nq], AF.Exp, scale=sc), [sk], [pk])

                def PV(kt, it0=it, v_=v_, vk=vk, nq=nq, diff=diff, nkt=nkt):
                    p_, pk = pT[(it0 + kt) % 3], "pT%d" % ((it0 + kt) % 3)
                    for g in range(2):
                        if diff:
                            self.mm(O[g][:, :nq], v_[:, kt, 0:128], p_[:, g, :nq], kt == 0, kt == nkt - 1, [vk, pk], ["o%d" % g])
                        else:
                            self.mm(O[g][0:65, :nq], v_[:, kt, g * 65:(g + 1) * 65], p_[:, g, :nq], kt == 0, kt == nkt - 1, [vk, pk], ["o%d" % g])
                    if diff:
                        a_ = accs[kt % 2]
                        ka, kb = "accA%d" % (kt % 2), "accB%d" % (kt % 2)
                        parts = (("dve", a_[:, :, :nq], p_[:, :, :nq], ka),)
                        for (en, ao, po, kx) in parts:
                            if kt < 2:
                                P.op(en, lambda e, ao=ao, po=po: e.tensor_copy(ao, po), [pk], [kx])
                            else:
                                P.op(en, lambda e, ao=ao, po=po: e.tensor_tensor(ao, ao, po, ALU.add), [pk, kx], [kx])

                S_exp(0)
                if nkt > 1:
                    S_exp(1)
                for kt in range(nkt):
                    if kt + 2 < nkt:
                        S_exp(kt + 2)
                    PV(kt)
                it += nkt
                if diff:
                    for g in range(2):
                        for a2 in range(2):
                            self.mm(L[g][:, :nq], self.onesf[:, :], accs[a2][:, g, :nq], a2 == 0, a2 == 1,
                                    ["onesf", "accA%d" % a2, "accB%d" % a2], ["sps2"])
                    for g in range(2):
                        P.op("act", lambda e, g=g, nq=nq: e.activation(rg[:, :nq], L[g][:, :nq], AF.Ln), ["sps2"], ["rg"])
                        P.op("act", lambda e, nq=nq: e.activation(rg[:, :nq], rg[:, :nq], AF.Exp, scale=-1.0), ["rg"], ["rg"])
                        P.op("dve", lambda e, g=g, nq=nq: e.tensor_tensor(og[g][:, :nq], O[g][:, :nq], rg[:, :nq], ALU.mult),
                             ["o%d" % g, "rg"], ["og%d" % g])
                    P.op("dve", lambda e, nq=nq: e.scalar_tensor_tensor(dd[:, :nq], og[1][:, :nq], self.nlam[:, i:i + 1], og[0][:, :nq],
                                                                       ALU.mult, ALU.add), ["og0", "og1", "nlam"], ["dd"])
                    P.op("act", lambda e, nq=nq: e.activation(sqd[:, :nq], dd[:, :nq], AF.Square), ["dd"], ["sqd"])
                    self.mm(L[0][:, :nq], self.cm[:, 3, :], sqd[:, :nq], True, True, ["cm", "sqd"], ["sps2"])
                    P.op("act", lambda e, nq=nq: e.activation(rsd[:, :nq], L[0][:, :nq], AF.Ln, bias=self.epsc[:, 0:1], scale=1.0),
                         ["sps2", "epsc"], ["rsd"])
                    P.op("act", lambda e, nq=nq: e.activation(rsd[:, :nq], rsd[:, :nq], AF.Exp, scale=-0.5), ["rsd"], ["rsd"])
                    sg_, sgk = stg[nst % 4], "stg%d" % (nst % 4)
                    nst += 1
                    P.op("dve", lambda e, nq=nq, sg_=sg_: e.scalar_tensor_tensor(sg_[:, :nq], dd[:, :nq], self.acl[:, i, 12:13], rsd[:, :nq],
                                                                                ALU.mult, ALU.mult), ["dd", "rsd", "acl"], [sgk])
                    self.dma("sp", self.mT[u][:, c0:c0 + nq], sg_[:, :nq], [sgk], [], sgk)
                else:
                    for h in range(2):
                        P.op("act", lambda e, h=h, nq=nq: e.activation(rrow[64:65, :nq], O[h][64:65, :nq], AF.Ln), ["o%d" % h], ["rrow"])
                        P.op("act", lambda e, nq=nq: e.activation(rrow[64:65, :nq], rrow[64:65, :nq], AF.Exp, scale=-1.0), ["rrow"], ["rrow"])
                        self.mm(L[h][0:64, :nq], sel[:, :], rrow[:, :nq], True, True, ["sel", "rrow"], ["sps2"])
                        P.op("act", lambda e, h=h, nq=nq: e.copy(rbs[0:64, :nq], L[h][0:64, :nq]), ["sps2"], ["rbs"])
                        sg_, sgk = stg[nst % 4], "stg%d" % (nst % 4)
                        nst += 1
                        P.op("dve", lambda e, h=h, nq=nq, sg_=sg_: e.tensor_tensor(sg_[0:64, :nq], O[h][0:64, :nq], rbs[0:64, :nq], ALU.mult),
                             ["o%d" % h, "rbs"], [sgk])
                        self.dma("sp", self.mT[u][h * 64:(h + 1) * 64, c0:c0 + nq], sg_[0:64, :nq], [sgk], [], sgk)

    def merge_pass(self, l, blocks, src, dst, prefetch=None):
        P = self.P
        i = l // 2
        with ExitStack() as st:
            wo = self.sb(st, "wo", [128, 8, D], BF16)
            s_o = self.w_out[i].rearrange("(k p) n -> p k n", p=128)
            for k in range(8):
                self.dma("pool", wo[:, k, :], s_o[:, k, :], [], ["wo"], "wo")
            if prefetch is not None:
                prefetch()
            xts = [self.sb(st, "xt%d" % k, [128, 8, TB]) for k in range(2)]
            mts = [self.sb(st, "mt%d" % k, [128, 8, TB], BF16) for k in range(2)]
            xos = [self.sb(st, "xo%d" % k, [128, 8, TB]) for k in range(2)]
            ys = [self.ps(st, "y%d" % k) for k in range(2)]

            def load(it):
                b = blocks[it]
                self.dma("sp", xts[it % 2][:, :, :], src[b], [], ["xt%d" % (it % 2)], "xt%d" % (it % 2))
                self.dma("sp", mts[it % 2][:, :, :], self.mT[:, :, b * TB:(b + 1) * TB].rearrange("c p t -> p c t"),
                         [], ["mt%d" % (it % 2)], "mt%d" % (it % 2))

            load(0)
            for it, b in enumerate(blocks):
                if it + 1 < len(blocks):
                    load(it + 1)
                s = 1 if b == 0 else 0
                xt, xk = xts[it % 2], "xt%d" % (it % 2)
                mt, mk = mts[it % 2], "mt%d" % (it % 2)
                xo, xok = xos[it % 2], "xo%d" % (it % 2)
                for d in range(8):
                    yp, yk = ys[d % 2], "y%d" % (d % 2)
                    for c in range(8):
                        self.mm(yp[:, :TB], wo[:, c, d * 128:(d + 1) * 128], mt[:, c, :], c == 0, c == 7, ["wo", mk], [yk])
                    G = self.modc[:, l, s, 5, d:d + 1]
                    o = xo[:, d, :]
                    xi = xt[:, d, :]
                    P.op("dve", lambda e, o=o, yp=yp, G=G, xi=xi: e.scalar_tensor_tensor(o, yp[:, :TB], G, xi, ALU.mult, ALU.add),
                         [yk, "modc", xk], [xok])
                self.dma("sp", dst[b], xo[:, :, :], [xok], [], xok)


def _blocked(X):
    T = X.shape[0]
    return np.ascontiguousarray(X.reshape(T // TB, TB, 8, 128).transpose(0, 3, 2, 1))


def _unblocked(Y):
    nb = Y.shape[0]
    return np.ascontiguousarray(Y.transpose(0, 3, 2, 1).reshape(nb * TB, D))


def _cols(v):
    v = np.asarray(v, np.float32)
    lead = v.shape[:-1]
    r = v.reshape(*lead, 8, 128)
    r = np.moveaxis(r, -1, 0)
    return np.ascontiguousarray(r)


def const_tables(NLB):
    NB = 1 + NLB
    T = NB * TB
    S = NLB * TB
    cm = np.zeros((128, 9, 128), np.float32)
    cm[:, 8, :] = 1.0
    cm[:, 0, :] = 1.0 / 1024
    cm[:, 1, :] = 1.0 / 384
    cm[:, 2, :] = 1.0 / 256
    cm[:, 3, :] = 1.0 / 128
    for g in range(2):
        cm[g * 64:(g + 1) * 64, 4, g * 64:(g + 1) * 64] = 1.0 / 64
    for g in range(4):
        cm[g * 32:(g + 1) * 32, 5, g * 32:(g + 1) * 32] = 1.0 / 32
    for m in range(128):
        d = m % 64
        half = (d % 32) // 16
        partner = m + 16 if half == 0 else m - 16
        cm[partner, 6, m] = 1.0
    for m in range(128):
        d = m % 32
        half = (d % 16) // 8
        partner = m + 8 if half == 0 else m - 8
        cm[partner, 7, m] = 1.0
    invc = np.zeros((NB, 4, TB), np.float32)
    for b in range(NB):
        L = CTX if b == 0 else S
        t = np.arange(TB) + (0 if b == 0 else (b - 1) * TB)
        for gi, w in enumerate((2, 4, 8, 16)):
            lo = np.clip(t - w // 2, 0, L)
            hi = np.clip(t + (w - w // 2), 0, L)
            invc[b, gi] = 1.0 / (hi - lo).astype(np.float32)
    rt = np.zeros((4, 128, T), np.float32)
    rt[0, :, :] = 1.0
    rt[2, :, :] = 1.0
    tt = np.arange(S)
    rows = (tt // GRID_W).astype(np.float32)
    cols = (tt % GRID_W).astype(np.float32)
    for ti, dim in ((0, 64), (2, 32)):
        q = dim // 4
        freqs = (10000.0 ** (-np.arange(q, dtype=np.float32) / q)).astype(np.float32)
        for p in range(128):
            d = p % dim
            axis = d // (dim // 2)
            half = (d % (dim // 2)) // q
            f = d % q
            pos = rows if axis == 0 else cols
            ang = (pos * freqs[f]).astype(np.float32)
            rt[ti, p, CTX:] = np.cos(ang)
            rt[ti + 1, p, CTX:] = np.sin(ang) * (-1.0 if half == 0 else 1.0)
    return cm, invc, rt


def host_inputs(inp, b, NLB, consts):
    cm, invc, rt = consts
    S = NLB * TB
    f32 = np.float32
    X = np.concatenate([np.asarray(inp["ctx"][b], f32), np.asarray(inp["x"][b][:S], f32)], axis=0)
    m = {}
    m["xin"] = _blocked(X)
    cc = np.stack([np.asarray(inp["c"][b], f32), np.asarray(inp["c_ctx"], f32)], axis=0)
    m["cT"] = np.ascontiguousarray(cc.reshape(2, 8, 128).transpose(2, 1, 0))
    m["mod_w"] = np.asarray(inp["mod_w"], f32)
    m["mod_bT"] = np.ascontiguousarray(np.asarray(inp["mod_b"], f32).reshape(4, 72, 128).transpose(2, 0, 1))
    m["gT"] = np.ascontiguousarray(np.asarray(inp["norm_g"], f32).reshape(4, 3, 8, 128).transpose(3, 0, 1, 2))
    m["pscl"] = np.ascontiguousarray(np.asarray(inp["pool_scale"], f32).reshape(2, 8, 128).transpose(2, 0, 1))
    m["ffn_w_gate"] = np.asarray(inp["ffn_w_gate"], f32)
    m["ffn_w_up"] = np.asarray(inp["ffn_w_up"], f32)
    m["ffn_w_down"] = np.asarray(inp["ffn_w_down"], f32)
    m["pool_w"] = np.asarray(inp["pool_w"], f32)
    m["invc"] = invc
    m["attn_w_in"] = np.asarray(inp["attn_w_in"], f32)
    wq = np.asarray(inp["mla_w_q_b"], f32).reshape(2, 384, 8, 96)
    m["w_qb"] = np.ascontiguousarray(np.concatenate([wq[..., :64].reshape(2, 384, 512), wq[..., 64:].reshape(2, 384, 256)], axis=-1))
    wkv = np.asarray(inp["mla_w_kv_b"], f32).reshape(2, 256, 8, 128)
    m["w_kvb"] = np.ascontiguousarray(np.concatenate([wkv[..., :64].reshape(2, 256, 512), wkv[..., 64:].reshape(2, 256, 512)], axis=-1))
    m["attn_w_out"] = np.asarray(inp["attn_w_out"], f32)
    ac = np.zeros((128, 2, 16), f32)
    for i in range(2):
        ac[:, i, 0] = np.tile(np.asarray(inp["diff_qk_g"][i, 0], f32), 2)
        ac[:, i, 1] = np.tile(np.asarray(inp["diff_qk_g"][i, 1], f32), 2)
        ac[:, i, 2] = np.tile(np.asarray(inp["mla_nope_g"][i, 0], f32), 2)
        ac[:, i, 3] = np.tile(np.asarray(inp["mla_nope_g"][i, 1], f32), 2)
        ac[:, i, 4] = np.tile(np.asarray(inp["mla_rope_g"][i, 0], f32), 4)
        ac[:, i, 5] = np.tile(np.asarray(inp["mla_rope_g"][i, 1], f32), 4)
        ac[:, i, 6:9] = np.asarray(inp["mla_q_a_g"][i], f32).reshape(3, 128).T
        ac[:, i, 9:11] = np.asarray(inp["mla_kv_a_g"][i], f32).reshape(2, 128).T
        ac[:, i, 11] = np.asarray(inp["diff_subln_g"][i], f32)
    m["acols"] = ac
    m["lamv"] = np.ascontiguousarray(np.asarray(inp["diff_lambda"], f32).reshape(2, 1, 256))
    m["ropet"] = rt
    m["cmat"] = cm
    return m


_CACHE = {}


def run(inp, NLB, layers, stop=None, cores=N_CORES):
    key = (NLB, tuple(layers), stop)
    if key not in _CACHE:
        _CACHE[key] = Builder(NLB, layers, stop).build()
    nc = _CACHE[key]
    consts = const_tables(NLB)
    in_maps = [host_inputs(inp, b, NLB, consts) for b in range(cores)]
    res = run_bass_kernel_spmd(nc, in_maps, core_ids=list(range(cores)))
    outs = [_unblocked(np.asarray(r["out"])) for r in res.results]
    return np.stack(outs, axis=0)


def kernel(**inputs):
    return run(inputs, 32, (0, 1, 2, 3)).astype(np.float32)
```

```python
import math
from contextlib import ExitStack

import numpy as np
import concourse.bass as bass
import concourse.mybir as mybir
from concourse.bass_utils import run_bass_kernel_spmd

F32 = mybir.dt.float32
BF16 = mybir.dt.bfloat16
AF = mybir.ActivationFunctionType
ALU = mybir.AluOpType

D = 1024
DFF = 2816
NF = DFF // 128
TB = 256
CTX = 256
GRID_W = 64
EPS = 1e-6
IN_W = 2208
N_CORES = 8

ENGS = ("pe", "act", "dve", "pool", "sp")


class _Op:
    __slots__ = ("eng", "fn", "deps", "dma_waits", "signal", "sigval", "is_dma", "key", "ep")

    def __init__(self, eng, fn, is_dma, key):
        self.ep = 0
        self.eng = eng
        self.fn = fn
        self.deps = []
        self.dma_waits = []
        self.signal = False
        self.sigval = 0
        self.is_dma = is_dma
        self.key = key


class Prog:
    def __init__(self, nc, stack):
        self.nc = nc
        self.st = stack
        self.ops = {e: [] for e in ENGS}
        self.last_w = {}
        self.readers = {}
        self.dma_cnt = {}
        self.nops = 0
        self.epoch = 0

    def _add_dep(self, op, d):
        if d is None or d is op:
            return
        if d.is_dma:
            op.dma_waits.append((d.key, self.dma_cnt[d.key] * 16))
        else:
            if d.eng == "pe" and op.eng == "pe":
                return
            d.signal = True
            op.deps.append(d)

    def op(self, eng, fn, reads=(), writes=(), dma=None):
        o = _Op(eng, fn, dma is not None, dma)
        o.ep = self.epoch if eng == "pe" else 0
        for r in reads:
            self._add_dep(o, self.last_w.get(r))
        for w in writes:
            self._add_dep(o, self.last_w.get(w))
            for rd in list(self.readers.get(w, {}).values()):
                self._add_dep(o, rd)
        rk = ("D", dma) if dma is not None else eng
        for r in reads:
            self.readers.setdefault(r, {})[rk] = o
        for w in writes:
            self.last_w[w] = o
            self.readers[w] = {}
        if dma is not None:
            self.dma_cnt[dma] = self.dma_cnt.get(dma, 0) + 1
        self.ops[eng].append(o)
        self.nops += 1
        return o

    def barrier(self):
        lasts = []
        for e in ENGS:
            for o in reversed(self.ops[e]):
                if not o.is_dma and o.fn is not None:
                    lasts.append(o)
                    break
        dmas = [(k, c * 16) for k, c in self.dma_cnt.items()]
        for e in ENGS:
            f = _Op(e, None, False, None)
            for d in lasts:
                if d.eng != e:
                    d.signal = True
                    f.deps.append(d)
            f.dma_waits = list(dmas)
            self.ops[e].append(f)
        self.last_w = {}
        self.readers = {}
        self.epoch += 1

    def emit(self):
        nc = self.nc
        sems = {}
        for e in ENGS:
            for ep in range(self.epoch + 1 if e == "pe" else 1):
                sems[("E", e, ep)] = self.st.enter_context(nc.semaphore("s_%s%d" % (e, ep)))
        for i, k in enumerate(self.dma_cnt):
            sems[("D", k)] = self.st.enter_context(nc.semaphore("d%d" % i))
        for e in ENGS:
            c = {}
            for o in self.ops[e]:
                if o.signal:
                    c[o.ep] = c.get(o.ep, 0) + 1
                    o.sigval = c[o.ep]
        block = self.st.enter_context(nc.Block())

        def run(e, eng):
            waited = {}
            for o in self.ops[e]:
                need = {}
                for d in o.deps:
                    s = ("E", d.eng, d.ep)
                    if need.get(s, 0) < d.sigval:
                        need[s] = d.sigval
                for k, v in o.dma_waits:
                    s = ("D", k)
                    if need.get(s, 0) < v:
                        need[s] = v
                for s, v in need.items():
                    if waited.get(s, 0) < v:
                        eng.wait_ge(sems[s], v)
                        waited[s] = v
                if o.fn is None:
                    continue
                ins = o.fn(eng)
                if o.is_dma:
                    ins.then_inc(sems[("D", o.key)], 16)
                elif o.signal:
                    ins.then_inc(sems[("E", e, o.ep)], 1)

        @block.tensor
        def _(eng):
            run("pe", eng)

        @block.scalar
        def _(eng):
            run("act", eng)

        @block.vector
        def _(eng):
            run("dve", eng)

        @block.gpsimd
        def _(eng):
            run("pool", eng)

        @block.sync
        def _(eng):
            run("sp", eng)


def lam_init_of(layer):
    return 0.8 - 0.6 * math.exp(-0.3 * layer)


class Builder:
    def __init__(self, NLB, layers, stop=None):
        self.layers = list(layers)
        n_layers = 4
        self.NLB = NLB
        self.NB = 1 + NLB
        self.T = self.NB * TB
        self.NKT = 2 * self.NB
        self.n_layers = n_layers
        self.stop = stop
        self.nc = bass.Bass("TRN2", target_bir_lowering=False)
        self.root = ExitStack()
        self.P = Prog(self.nc, self.root)
        self.uid = 0

    def din(self, name, shape, dt=F32):
        return self.nc.dram_tensor(name, list(shape), dt, kind="ExternalInput").ap()

    def dscr(self, name, shape, dt=F32):
        return self.nc.dram_tensor(name, list(shape), dt).ap()

    def sb(self, st, name, shape, dt=F32):
        self.uid += 1
        return st.enter_context(self.nc.sbuf_tensor("%s_%d" % (name, self.uid), list(shape), dt))

    def ps(self, st, name, shape=(128, 512), dt=F32):
        self.uid += 1
        return st.enter_context(self.nc.psum_tensor("%s_%d" % (name, self.uid), list(shape), dt))

    def dma(self, eng, out, in_, reads, writes, key):
        self.P.op(eng, lambda e: e.dma_start(out=out, in_=in_), reads=reads, writes=writes, dma=key)

    def mm(self, out, lhsT, rhs, start, stop, reads, writes):
        self.P.op("pe", lambda e: e.matmul(out, lhsT, rhs, start=start, stop=stop), reads=reads, writes=writes)

    def declare(self):
        NB, NLB, T = self.NB, self.NLB, self.T
        self.xin = self.din("xin", [NB, 128, 8, TB])
        self.cT = self.din("cT", [128, 8, 2])
        self.mod_w = self.din("mod_w", [4, D, 9 * D])
        self.mod_bT = self.din("mod_bT", [128, 4, 72])
        self.gT = self.din("gT", [128, 4, 3, 8])
        self.pscl = self.din("pscl", [128, 2, 8])
        self.wg = self.din("ffn_w_gate", [4, 2, D, DFF])
        self.wu = self.din("ffn_w_up", [4, 2, D, DFF])
        self.wd = self.din("ffn_w_down", [4, 2, DFF, D])
        self.pool_w = self.din("pool_w", [2, 4, 256, 256])
        self.invc = self.din("invc", [NB, 4, TB])
        self.w_in = self.din("attn_w_in", [2, D, IN_W])
        self.w_qb = self.din("w_qb", [2, 384, 768])
        self.w_kvb = self.din("w_kvb", [2, 256, 1024])
        self.w_out = self.din("attn_w_out", [2, D, D])
        self.acols = self.din("acols", [128, 2, 16])
        self.lamv = self.din("lamv", [2, 1, 256])
        self.ropet = self.din("ropet", [4, 128, T])
        self.cmat = self.din("cmat", [128, 9, 128])
        self.out = self.nc.dram_tensor("out", [NLB, 128, 8, TB], F32, kind="ExternalOutput").ap()
        self.xs = [self.dscr("xs0", [NB, 128, 8, TB]), self.dscr("xs1", [NB, 128, 8, TB])]
        self.dqT = self.dscr("dqT", [4, 128, T], BF16)
        self.dkT = self.dscr("dkT", [4, 128, T], BF16)
        self.dV = self.dscr("dV", [4, 128, self.NKT, 128], BF16)
        self.mqT = self.dscr("mqT", [8, 96, T], BF16)
        self.mknT = self.dscr("mknT", [8, 64, T], BF16)
        self.krT = self.dscr("krT", [32, T], BF16)
        self.mV = self.dscr("mV", [4, 128, self.NKT, 2, 65], BF16)
        self.mT = self.dscr("mT", [8, 128, T], BF16)

    def build(self):
        self.declare()
        st = self.root
        self.modc = self.sb(st, "modc", [128, 4, 2, 9, 8])
        self.cm = self.sb(st, "cm", [128, 9, 128], BF16)
        self.onesf = self.sb(st, "onesf", [128, 128])
        self.acl = self.sb(st, "acl", [128, 2, 16])
        self.nlam = self.sb(st, "nlam", [128, 2])
        self.epsc = self.sb(st, "epsc", [128, 1])
        w1st = ExitStack()
        W1 = self.alloc_ffn_w(w1st)
        self.prologue(prefetch=lambda: self.issue_ffn_w(W1, self.layers[0], 0))
        self.P.barrier()
        src = self.xin
        cur = 0
        allb = list(range(self.NB))
        latb = list(range(1, self.NB))
        layers = self.layers
        for li, l in enumerate(layers):
            last_layer = li == len(layers) - 1
            even = l % 2 == 0
            bl = allb if l < 3 else latb
            bq = allb if l < 2 else latb
            dst = self.xs[cur]
            self.ffn_pass(l, 0, bl, src, dst, None, W=(W1 if li == 0 else None))
            self.P.barrier()
            if li == 0:
                w1st.close()
            src = dst
            cur ^= 1
            if self.stop == "ffn1":
                break
            if even:
                self.qkv_pass(l, bl, src)
                self.P.barrier()
                if self.stop == "qkv":
                    break
                self.attn_pass(l, with_ctx=(l < 2))
                self.P.barrier()
                if self.stop == "attn":
                    break
                with ExitStack() as wst:
                    W2 = self.alloc_ffn_w(wst)
                    dst = self.xs[cur]
                    self.merge_pass(l, bq, src, dst, prefetch=lambda W2=W2, l=l: self.issue_ffn_w(W2, l, 1))
                    self.P.barrier()
                    src = dst
                    cur ^= 1
                    if self.stop == "mix":
                        break
                    final = last_layer and self.stop is None
                    dst = self.out if final else self.xs[cur]
                    self.ffn_pass(l, 1, latb if final else bq, src, dst, final, W=W2)
                    self.P.barrier()
                    src = dst
                    cur ^= 1
            else:
                with ExitStack() as wst:
                    W2 = self.alloc_ffn_w(wst)
                    dst = self.xs[cur]
                    self.poolffn_pass(l, bq, src, dst, False, prefetch=lambda W2=W2, l=l: self.issue_ffn_w(W2, l, 1))
                    self.P.barrier()
                    src = dst
                    cur ^= 1
                    if self.stop == "mix":
                        break
                    final = last_layer and self.stop is None
                    dst = self.out if final else self.xs[cur]
                    self.ffn_pass(l, 1, latb if final else bq, src, dst, final, W=W2)
                    self.P.barrier()
                    src = dst
                    cur ^= 1
        if self.stop is not None:
            with ExitStack() as st2:
                t = self.sb(st2, "cp", [128, 8, TB])
                for b in latb:
                    self.dma("sp", t[:, :, :], src[b], [], ["cp"], "cp")
                    self.dma("sp", self.out[b - 1], t[:, :, :], ["cp"], [], "cp")
                self.P.barrier()
        self.P.emit()
        return self.nc

    def prologue(self, prefetch=None):
        P = self.P
        with ExitStack() as st:
            cT = self.sb(st, "cTt", [128, 8, 2])
            sT = self.sb(st, "sTt", [128, 8, 2])
            mb = self.sb(st, "mbt", [128, 4, 72])
            gT = self.sb(st, "gTt", [128, 4, 3, 8])
            psc = self.sb(st, "psct", [128, 2, 8])
            cmf = self.sb(st, "cmf", [128, 9, 128])
            mw = [self.sb(st, "mw%d" % i, [128, 8, 768]) for i in range(2)]
            mraw = self.sb(st, "mraw", [128, 72, 2])
            mo = [self.ps(st, "mo%d" % i) for i in range(2)]
            self.dma("sp", cT[:, :, :], self.cT, [], ["cT"], "c0")
            self.dma("sp", mb[:, :, :], self.mod_bT, [], ["mb"], "c1")
            self.dma("sp", gT[:, :, :, :], self.gT, [], ["gT"], "c2")
            self.dma("sp", psc[:, :, :], self.pscl, [], ["psc"], "c3")
            self.dma("sp", cmf[:, :, :], self.cmat, [], ["cmf"], "c4")
            self.dma("sp", self.acl[:, :, :], self.acols, [], ["acl"], "c5")
            if prefetch is not None:
                prefetch()
            P.op("dve", lambda e: e.tensor_copy(self.cm[:, :, :], cmf[:, :, :]), ["cmf"], ["cm"])
            P.op("dve", lambda e: e.memset(self.epsc[:, :], EPS), [], ["epsc"])
            P.op("act", lambda e: e.activation(sT[:, :, :], cT[:, :, :], AF.Silu), ["cT"], ["sT"])
            P.op("pool", lambda e: e.memset(self.onesf[:, :], 1.0), [], ["onesf"])
            lt = self.sb(st, "lt", [1, 2, 256])
            pr = self.sb(st, "pr", [1, 2, 2, 64])
            sm = self.sb(st, "sm", [1, 8])
            self.dma("sp", lt[:, :, :], self.lamv.rearrange("i o n -> o i n"), [], ["lt"], "c6")
            for i in range(2):
                li_ = lam_init_of(2 * i)
                for h in range(2):
                    a0 = lt[0:1, i, (2 * h) * 64:(2 * h + 1) * 64]
                    a1 = lt[0:1, i, (2 * h + 1) * 64:(2 * h + 2) * 64]
                    o = pr[0:1, i, h, :]
                    P.op("dve", lambda e, o=o, a0=a0, a1=a1: e.tensor_tensor(o, a0, a1, ALU.mult), ["lt"], ["pr"])
                P.op("dve", lambda e, i=i: e.reduce_sum(sm[0:1, 4 * i:4 * i + 2], pr[0:1, i, :, :], mybir.AxisListType.X),
                     ["pr"], ["sm"])
                P.op("act", lambda e, i=i: e.activation(sm[0:1, 4 * i:4 * i + 2], sm[0:1, 4 * i:4 * i + 2], AF.Exp), ["sm"], ["sm"])
                P.op("dve", lambda e, i=i: e.tensor_tensor(sm[0:1, 4 * i + 2:4 * i + 3], sm[0:1, 4 * i + 1:4 * i + 2],
                                                          sm[0:1, 4 * i:4 * i + 1], ALU.subtract), ["sm"], ["sm"])
                P.op("dve", lambda e, i=i, li_=li_: e.tensor_scalar(sm[0:1, 4 * i + 3:4 * i + 4], sm[0:1, 4 * i + 2:4 * i + 3],
                                                                   -li_, None, ALU.add), ["sm"], ["sm"])
                self.mm(mo[0][:, 100 + i:101 + i], self.onesf[0:1, :], sm[0:1, 4 * i + 3:4 * i + 4], True, True,
                        ["onesf", "sm"], ["mo0"])
                P.op("dve", lambda e, i=i: e.tensor_copy(self.nlam[:, i:i + 1], mo[0][:, 100 + i:101 + i]), ["mo0"], ["nlam"])
                P.op("dve", lambda e, i=i, li_=li_: e.tensor_scalar(self.acl[:, i, 12:13], self.acl[:, i, 11:12], 1.0 - li_, None, ALU.mult),
                     ["acl"], ["acl"])
            for l in self.layers:
                for q in range(12):
                    buf = mw[q % 2]
                    bk = "mw%d" % (q % 2)
                    src = self.mod_w[l, :, q * 768:(q + 1) * 768].rearrange("(k p) n -> p k n", p=128)
                    for k2 in range(2):
                        self.dma("sp" if k2 == 0 else "act", buf[:, k2 * 4:(k2 + 1) * 4, :], src[:, k2 * 4:(k2 + 1) * 4, :], [], [bk], bk)
                    pt = mo[q % 2]
                    pk = "mo%d" % (q % 2)
                    for jj in range(6):
                        for k in range(8):
                            self.mm(pt[:, jj * 2:jj * 2 + 2], buf[:, k, jj * 128:(jj + 1) * 128], sT[:, k, :],
                                    k == 0, k == 7, [bk, "sT"], [pk])
                    o = mraw[:, q * 6:(q + 1) * 6, :]
                    i0 = pt[:, 0:12].rearrange("p (a b) -> p a b", b=2)
                    i1 = mb[:, l, q * 6:(q + 1) * 6].unsqueeze(2).to_broadcast([128, 6, 2])
                    P.op("dve", lambda e, o=o, i0=i0, i1=i1: e.tensor_tensor(o, i0, i1, ALU.add), [pk, "mb"], ["mraw"])
                for s in range(2):
                    for i in range(3):
                        sc = mraw[:, (3 * i + 1) * 8:(3 * i + 2) * 8, s]
                        sh = mraw[:, (3 * i) * 8:(3 * i + 1) * 8, s]
                        ga = mraw[:, (3 * i + 2) * 8:(3 * i + 3) * 8, s]
                        A = self.modc[:, l, s, 3 * i + 0, :]
                        Bc = self.modc[:, l, s, 3 * i + 1, :]
                        G = self.modc[:, l, s, 3 * i + 2, :]
                        g = gT[:, l, i, :]
                        P.op("dve", lambda e, A=A, sc=sc, g=g: e.scalar_tensor_tensor(A, sc, 1.0, g, ALU.add, ALU.mult),
                             ["mraw", "gT"], ["modc"])
                        P.op("dve", lambda e, Bc=Bc, sh=sh: e.tensor_copy(Bc, sh), ["mraw"], ["modc"])
                        if i == 1 and l % 2 == 1:
                            pcl = psc[:, l // 2, :]
                            P.op("dve", lambda e, G=G, ga=ga, pcl=pcl: e.tensor_tensor(G, ga, pcl, ALU.mult),
                                 ["mraw", "psc"], ["modc"])
                        else:
                            f = 1.0 if i == 1 else 0.5
                            P.op("dve", lambda e, G=G, ga=ga, f=f: e.tensor_scalar(G, ga, f, None, ALU.mult),
                                 ["mraw"], ["modc"])

    def norm_mod(self, tl, xt, xk, n, l, s, i, out, outk, fp32_out=False):
        P = self.P
        sq, rstd, tmp, ssp = tl["sq"], tl["rstd"], tl["ntmp"], tl["ss"]
        P.op("pool", lambda e: e.tensor_tensor(sq[:, :, :n], xt, xt, ALU.mult), [xk], ["sq"])
        for j in range(8):
            self.mm(ssp[:, :n], self.cm[:, 0, :], sq[:, j, :n], j == 0, j == 7, ["sq", "cm"], ["ss"])
        P.op("act", lambda e: e.activation(rstd[:, :n], ssp[:, :n], AF.Ln, bias=self.epsc[:, 0:1], scale=1.0),
             ["ss", "epsc"], ["rstd"])
        P.op("act", lambda e: e.activation(rstd[:, :n], rstd[:, :n], AF.Exp, scale=-0.5), ["rstd"], ["rstd"])
        rb = rstd[:, :n].unsqueeze(1).to_broadcast([128, 8, n])
        P.op("dve", lambda e: e.tensor_tensor(tmp[:, :, :n], xt, rb, ALU.mult), [xk, "rstd"], ["ntmp"])
        for j in range(8):
            A = self.modc[:, l, s, 3 * i, j:j + 1]
            Bc = self.modc[:, l, s, 3 * i + 1, j:j + 1]
            o = out[:, j, :n]
            ti = tmp[:, j, :n]
            eng = "dve" if (j % 2 == 0) else "pool"
            P.op(eng, lambda e, o=o, ti=ti, A=A, Bc=Bc: e.tensor_scalar(o, ti, A, Bc, ALU.mult, ALU.add),
                 ["ntmp", "modc"], [outk])

    def alloc_ffn_w(self, st):
        wg = self.sb(st, "wg", [128, 8, DFF], BF16)
        wu = self.sb(st, "wu", [128, 8, DFF], BF16)
        wd = self.sb(st, "wd", [128, NF, D], BF16)
        return wg, wu, wd

    def issue_ffn_w(self, W, l, w):
        wg, wu, wd = W
        sg_ = self.wg[l, w].rearrange("(k p) n -> p k n", p=128)
        su_ = self.wu[l, w].rearrange("(k p) n -> p k n", p=128)
        sd_ = self.wd[l, w].rearrange("(f p) n -> p f n", p=128)
        H = 11 * 128
        for (c0, c1, sfx) in ((0, H, "A"), (H, DFF, "B")):
            for k in range(8):
                self.dma("pool", wg[:, k, c0:c1], sg_[:, k, c0:c1], [], ["wg" + sfx], "wg" + sfx)
                self.dma("pool", wu[:, k, c0:c1], su_[:, k, c0:c1], [], ["wu" + sfx], "wu" + sfx)
        for f in range(0, NF, 2):
            self.dma("pool", wd[:, f:f + 2, :], sd_[:, f:f + 2, :], [], ["wd"], "wd")

    def load_ffn_w(self, st, l, w):
        W = self.alloc_ffn_w(st)
        self.issue_ffn_w(W, l, w)
        return W

    def ffn_tiles(self, st):
        tl = {}
        tl["sq"] = self.sb(st, "sq", [128, 8, TB + 16], BF16)
        tl["rstd"] = self.sb(st, "rstd", [128, TB + 16])
        tl["ntmp"] = self.sb(st, "ntmp", [128, 8, TB + 16])
        tl["ht"] = self.sb(st, "ht", [128, 8, TB], BF16)
        tl["acth"] = self.sb(st, "acth", [128, NF, TB], BF16)
        tl["sg"] = [self.sb(st, "sg%d" % i, [128, TB]) for i in range(2)]
        tl["xo"] = [self.sb(st, "xo%d" % i, [128, 8, TB]) for i in range(2)]
        tl["ss"] = self.ps(st, "ss")
        tl["gate"] = [self.ps(st, "gate%d" % i) for i in range(2)]
        tl["up"] = [self.ps(st, "up%d" % i) for i in range(2)]
        tl["y"] = [self.ps(st, "y%d" % i) for i in range(2)]
        return tl

    def ffn_B(self, tl, W):
        P = self.P
        wg, wu, wd = W
        ht, acth = tl["ht"], tl["acth"]
        for f in range(NF):
            gp, up_ = tl["gate"][f % 2], tl["up"][f % 2]
            gk, uk = "gate%d" % (f % 2), "up%d" % (f % 2)
            sfx = "A" if f < 11 else "B"
            for k in range(8):
                self.mm(gp[:, :TB], wg[:, k, f * 128:(f + 1) * 128], ht[:, k, :], k == 0, k == 7, ["wg" + sfx, "ht"], [gk])
            for k in range(8):
                self.mm(up_[:, :TB], wu[:, k, f * 128:(f + 1) * 128], ht[:, k, :], k == 0, k == 7, ["wu" + sfx, "ht"], [uk])
            sg = tl["sg"][f % 2]
            sk = "sg%d" % (f % 2)
            P.op("act", lambda e, sg=sg, gp=gp: e.activation(sg[:, :], gp[:, :TB], AF.Silu), [gk], [sk])
            o = acth[:, f, :]
            P.op("dve", lambda e, o=o, sg=sg, up_=up_: e.tensor_tensor(o, sg[:, :], up_[:, :TB], ALU.mult),
                 [sk, uk], ["acth"])

    def ffn_C(self, tl, W, xt, xk, l, s, i, it, dst_ap):
        P = self.P
        wg, wu, wd = W
        acth = tl["acth"]
        xo = tl["xo"][it % 2]
        xok = "xo%d" % (it % 2)
        for d in range(8):
            yp = tl["y"][d % 2]
            yk = "y%d" % (d % 2)
            for f in range(NF):
                self.mm(yp[:, :TB], wd[:, f, d * 128:(d + 1) * 128], acth[:, f, :], f == 0, f == NF - 1,
                        ["wd", "acth"], [yk])
            G = self.modc[:, l, s, 3 * i + 2, d:d + 1]
            o = xo[:, d, :]
            xi = xt[:, d, :]
            P.op("dve", lambda e, o=o, yp=yp, G=G, xi=xi: e.scalar_tensor_tensor(o, yp[:, :TB], G, xi, ALU.mult, ALU.add),
                 [yk, "modc", xk], [xok])
        self.dma("sp", dst_ap, xo[:, :, :], [xok], [], xok)

    def ffn_pass(self, l, w, blocks, src, dst, to_out, W=None):
        with ExitStack() as st:
            if W is None:
                W = self.load_ffn_w(st, l, w)
            tl = self.ffn_tiles(st)
            xts = [self.sb(st, "xt%d" % i, [128, 8, TB]) for i in range(2)]
            i = 0 if w == 0 else 2
            nb = len(blocks)

            def load(it):
                self.dma("sp", xts[it % 2][:, :, :], src[blocks[it]], [], ["xt%d" % (it % 2)], "xt%d" % (it % 2))

            def A(it):
                s = 1 if blocks[it] == 0 else 0
                self.norm_mod(tl, xts[it % 2][:, :, :], "xt%d" % (it % 2), TB, l, s, i, tl["ht"], "ht")

            load(0)
            if nb > 1:
                load(1)
            A(0)
            for it, b in enumerate(blocks):
                s = 1 if b == 0 else 0
                self.ffn_B(tl, W)
                if it + 1 < nb:
                    A(it + 1)
                self.ffn_C(tl, W, xts[it % 2][:, :, :], "xt%d" % (it % 2), l, s, i, it, dst[b - 1] if to_out else dst[b])
                if it + 2 < nb:
                    load(it + 2)

    def poolffn_pass(self, l, blocks, src, dst, to_out, prefetch=None):
        P = self.P
        NB = self.NB
        with ExitStack() as st:
            tl = {}
            tl["sq"] = self.sb(st, "sq", [128, 8, TB + 16], BF16)
            tl["rstd"] = self.sb(st, "rstd", [128, TB + 16])
            tl["ntmp"] = self.sb(st, "ntmp", [128, 8, TB + 16])
            tl["ss"] = self.ps(st, "ss")
            pw = self.sb(st, "pw", [128, 4, 2, 256], BF16)
            self.dma("pool", pw[:, :, :, :], self.pool_w[l // 2].rearrange("g (k p) n -> p g k n", p=128), [], ["pw"], "pw")
            if prefetch is not None:
                prefetch()
            xhs = [self.sb(st, "xh%d" % i, [128, 8, TB + 16]) for i in range(2)]
            hh = self.sb(st, "hh", [128, 8, TB + 16])
            ta = self.sb(st, "ta", [128, 2, TB + 16])
            tb_ = self.sb(st, "tb", [128, 2, TB + 16])
            ivc = self.sb(st, "ivc", [128, 4, TB])
            dT = self.sb(st, "dT", [128, 8, TB], BF16)
            xms = [self.sb(st, "xm%d" % i, [128, 8, TB]) for i in range(1)] * 2
            yp = self.ps(st, "yp")
            n = TB + 16
            def load(it):
                b = blocks[it]
                xh = xhs[it % 2]
                xk = "xh%d" % (it % 2)
                has_l = (b >= 2)
                has_r = (b >= 1 and b + 1 < NB)
                if not has_l:
                    P.op("pool", lambda e, xh=xh: e.memset(xh[:, :, 0:8], 0.0), [], [xk])
                if not has_r:
                    P.op("pool", lambda e, xh=xh: e.memset(xh[:, :, TB + 8:TB + 16], 0.0), [], [xk])
                self.dma("sp", xh[:, :, 8:TB + 8], src[b], [], [xk], xk)
                if has_l:
                    self.dma("sp", xh[:, :, 0:8], src[b - 1][:, :, TB - 8:TB], [], [xk], xk)
                if has_r:
                    self.dma("sp", xh[:, :, TB + 8:TB + 16], src[b + 1][:, :, 0:8], [], [xk], xk)

            load(0)
            for it, b in enumerate(blocks):
                xh = xhs[it % 2]
                xk = "xh%d" % (it % 2)
                s = 1 if b == 0 else 0
                has_l = (b >= 2)
                has_r = (b >= 1 and b + 1 < NB)
                if it + 1 < len(blocks):
                    load(it + 1)
                xm, xmk = xms[0], "xm0"
                self.dma("sp", ivc[:, :, :], self.invc[b].partition_broadcast(128), [], ["ivc"], "ivc")
                self.norm_mod(tl, xh[:, :, :], xk, n, l, s, 1, hh, "hh")
                if not has_l:
                    P.op("dve", lambda e: e.memset(hh[:, :, 0:8], 0.0), [], ["hh"])
                if not has_r:
                    P.op("dve", lambda e: e.memset(hh[:, :, TB + 8:TB + 16], 0.0), [], ["hh"])
                for gi in range(4):
                    c0 = 2 * gi
                    cur = hh[:, c0:c0 + 2, :]
                    curk = "hh"
                    ln = n
                    bufs = [(ta, "ta"), (tb_, "tb")]
                    for lev in range(gi + 1):
                        step = 1 << lev
                        nb_, nk = bufs[lev % 2]
                        ln2 = ln - step
                        a0 = cur[:, :, 0:ln2]
                        a1 = cur[:, :, step:step + ln2]
                        o = nb_[:, :, 0:ln2]
                        P.op("dve", lambda e, o=o, a0=a0, a1=a1: e.tensor_tensor(o, a0, a1, ALU.add), [curk], [nk])
                        cur, curk, ln = nb_, nk, ln2
                    half = 1 << gi
                    S = cur[:, :, 8 - half:8 - half + TB]
                    iv = ivc[:, gi, :].unsqueeze(1).to_broadcast([128, 2, TB])
                    ob, ok = bufs[(gi + 1) % 2]
                    o1 = ob[:, :, 0:TB]
                    P.op("dve", lambda e, o1=o1, S=S, iv=iv: e.tensor_tensor(o1, S, iv, ALU.mult), [curk, "ivc"], [ok])
                    hm = hh[:, c0:c0 + 2, 8:TB + 8]
                    od = dT[:, c0:c0 + 2, :]
                    P.op("dve", lambda e, od=od, o1=o1, hm=hm: e.tensor_tensor(od, o1, hm, ALU.subtract), [ok, "hh"], ["dT"])
                    for oc in range(2):
                        for k in range(2):
                            self.mm(yp[:, :TB], pw[:, gi, k, oc * 128:(oc + 1) * 128], dT[:, c0 + k, :], k == 0, k == 1,
                                    ["pw", "dT"], ["yp"])
                        G = self.modc[:, l, s, 5, c0 + oc:c0 + oc + 1]
                        o = xm[:, c0 + oc, :]
                        xi = xh[:, c0 + oc, 8:TB + 8]
                        P.op("dve", lambda e, o=o, G=G, xi=xi: e.scalar_tensor_tensor(o, yp[:, :TB], G, xi, ALU.mult, ALU.add),
                             ["yp", "modc", xk], [xmk])
                self.dma("act", dst[b], xm[:, :, :], [xmk], [], xmk)


    def qkv_pass(self, l, blocks, src):
        P = self.P
        i = l // 2
        T = self.T
        with ExitStack() as st:
            win = self.sb(st, "win", [128, 8, IN_W], BF16)
            wqb = self.sb(st, "wqb", [128, 3, 768], BF16)
            wkvb = self.sb(st, "wkvb", [128, 2, 1024], BF16)
            s_in = self.w_in[i].rearrange("(k p) n -> p k n", p=128)
            for k in range(8):
                self.dma("pool", win[:, k, :], s_in[:, k, :], [], ["win"], "win")
            self.dma("pool", wqb[:, :, :], self.w_qb[i].rearrange("(k p) n -> p k n", p=128), [], ["wqb"], "wqb")
            self.dma("pool", wkvb[:, :, :], self.w_kvb[i].rearrange("(k p) n -> p k n", p=128), [], ["wkvb"], "wkvb")
            tl = {}
            tl["sq"] = self.sb(st, "sq", [128, 8, TB], BF16)
            tl["rstd"] = self.sb(st, "rstd", [128, TB])
            tl["ntmp"] = self.sb(st, "ntmp", [128, 8, TB])
            tl["ss"] = self.ps(st, "ss")
            ht = self.sb(st, "ht", [128, 8, TB], BF16)
            xts = [self.sb(st, "xt%d" % k, [128, 8, TB]) for k in range(2)]
            rtbs = [self.sb(st, "rtb%d" % k, [128, 4, TB]) for k in range(2)]
            raws = [self.ps(st, "raw%d" % k) for k in range(3)]
            gsss = [self.ps(st, "gss%d" % k) for k in range(2)]
            gss = gsss[0]
            prm = self.ps(st, "prm")
            vts = [self.ps(st, "vt%d" % k) for k in range(1)]
            sqgs = [self.sb(st, "sqg%d" % k, [128, TB], BF16) for k in range(2)]
            rss = [self.sb(st, "rs%d" % k, [128, TB]) for k in range(2)]
            rs = rss[0]
            qns = [self.sb(st, "qn%d" % k, [128, TB], BF16) for k in range(2)]
            t1 = self.sb(st, "t1", [128, TB])
            t2 = self.sb(st, "t2", [128, TB])
            stq = [self.sb(st, "stq%d" % k, [128, TB], BF16) for k in range(3)]
            vst = [self.sb(st, "vst%d" % k, [128, 4, 128], BF16) for k in range(2)]
            mvst = [self.sb(st, "mvst%d" % k, [128, 8, 65], BF16) for k in range(2)]
            craw = self.sb(st, "craw", [128, 3, TB])
            csq = self.sb(st, "csq", [128, 3, TB], BF16)
            cn = self.sb(st, "cn", [128, 3, TB], BF16)
            for k in range(2):
                P.op("pool", lambda e, k=k: e.memset(mvst[k][:, :, 64:65], 1.0), [], ["mvst%d" % k])
            cnt = {"raw": 0, "qn": 0, "stq": 0, "vt": 0}

            def load(it):
                b = blocks[it]
                self.dma("sp", xts[it % 2][:, :, :], src[b], [], ["xt%d" % (it % 2)], "xt%d" % (it % 2))
                self.dma("sp", rtbs[it % 2][:, :, :], self.ropet[:, :, b * TB:(b + 1) * TB].rearrange("a p t -> p a t"),
                         [], ["rtb%d" % (it % 2)], "rtb%d" % (it % 2))

            def proj(W, wk, nk, col0, ncol, rhs_t, rhs_k):
                r = cnt["raw"] % 3
                cnt["raw"] += 1
                pt, pk = raws[r], "raw%d" % r
                for k in range(nk):
                    self.mm(pt[:ncol, :TB], W[:, k, col0:col0 + ncol], rhs_t[:, k, :], k == 0, k == nk - 1, [wk, rhs_k], [pk])
                return pt, pk

            def group_norm_a(pt, pk, np_, midx):
                r = cnt["qn"] % 2
                cnt["qn"] += 1
                sq_, sqk = sqgs[r], "sqg%d" % r
                gs_, gsk = gsss[r], "gss%d" % r
                P.op("act", lambda e: e.activation(sq_[:np_, :], pt[:np_, :TB], AF.Square), [pk], [sqk])
                self.mm(gs_[:np_, :TB], self.cm[:np_, midx, :np_], sq_[:np_, :], True, True, ["cm", sqk], [gsk])
                return r

            def group_norm_b(r, pt, pk, np_, gcol):
                qn, qk = qns[r], "qn%d" % r
                rs_, rsk = rss[r], "rs%d" % r
                gs_, gsk = gsss[r], "gss%d" % r
                P.op("act", lambda e: e.activation(rs_[:np_, :], gs_[:np_, :TB], AF.Ln, bias=self.epsc[:np_, 0:1], scale=1.0),
                     [gsk, "epsc"], [rsk])
                P.op("act", lambda e: e.activation(rs_[:np_, :], rs_[:np_, :], AF.Exp, scale=-0.5), [rsk], [rsk])
                P.op("dve", lambda e: e.scalar_tensor_tensor(qn[:np_, :], pt[:np_, :TB], gcol, rs_[:np_, :], ALU.mult, ALU.mult),
                     [pk, rsk, "acl"], [qk])
                return qn, qk

            def rope(qn, qk, np_, pidx, rtb, rk, ci, si, do_rope):
                if not do_rope:
                    return qn, qk
                r = cnt["stq"] % 3
                cnt["stq"] += 1
                o, ok = stq[r], "stq%d" % r
                self.mm(prm[:np_, :TB], self.cm[:np_, pidx, :np_], qn[:np_, :], True, True, ["cm", qk], ["prm"])
                P.op("pool", lambda e: e.tensor_tensor(t1[:np_, :], qn[:np_, :], rtb[:np_, ci, :], ALU.mult), [qk, rk], ["t1"])
                P.op("dve", lambda e: e.tensor_tensor(t2[:np_, :], prm[:np_, :TB], rtb[:np_, si, :], ALU.mult), ["prm", rk], ["t2"])
                P.op("pool", lambda e: e.tensor_tensor(o[:np_, :], t1[:np_, :], t2[:np_, :], ALU.add), ["t1", "t2"], [ok])
                return o, ok

            def gn_job(W, wk, nk, col0, ncol, rhs_t, rhs_k, midx, gcol, rope_args, stores):
                stt = {}

                def s0():
                    stt["pt"], stt["pk"] = proj(W, wk, nk, col0, ncol, rhs_t, rhs_k)

                def s1():
                    stt["r"] = group_norm_a(stt["pt"], stt["pk"], ncol, midx)

                def s1b():
                    stt["qn"], stt["qk"] = group_norm_b(stt["r"], stt["pt"], stt["pk"], ncol, gcol)

                def s2():
                    if rope_args is not None:
                        o, ok = rope(stt["qn"], stt["qk"], ncol, *rope_args)
                    else:
                        o, ok = stt["qn"], stt["qk"]
                    for (dap, p0, p1) in stores:
                        self.dma("sp", dap, o[p0:p1, :], [ok], [], ok)
                return [s0, s1, s1b, s2], False

            def vt_job(lhs_t, lhs_k, nk, tt, W, wk, c0, dst_tile, dst_key, dsl, dram_ap, sb_ap, d_):
                stt = {}

                def s0():
                    vp, vk = vts[0], "vt0"
                    for k in range(nk):
                        self.mm(vp[:, :512], lhs_t[:, k, tt * 128:(tt + 1) * 128], W[:, k, c0:c0 + 512], k == 0, k == nk - 1, [lhs_k, wk], [vk])
                    P.op("act", lambda e: e.copy(dsl, vp[:, :512].rearrange("p (h d) -> p h d", d=d_)), [vk], [dst_key])

                def s1():
                    self.dma("sp", dram_ap, sb_ap, [dst_key], [], dst_key)
                return [s0, s1], False

            def craw_job(c0, c):
                stt = {}

                def s0():
                    stt["pt"], stt["pk"] = proj(win, "win", 8, c0 + c * 128, 128, ht, "ht")

                def s1():
                    pt = stt["pt"]
                    P.op("act", lambda e: e.copy(craw[:, c, :], pt[:, :TB]), [stt["pk"]], ["craw"])
                    P.op("pool", lambda e: e.tensor_tensor(csq[:, c, :], craw[:, c, :], craw[:, c, :], ALU.mult), ["craw"], ["csq"])
                return [s0, s1], False

            def combine_job(nch, midx, g0):
                def s0():
                    for c in range(nch):
                        self.mm(gss[:, :TB], self.cm[:, midx, :], csq[:, c, :], c == 0, c == nch - 1, ["cm", "csq"], ["gss0"])
                    P.op("act", lambda e: e.activation(rs[:, :], gss[:, :TB], AF.Ln, bias=self.epsc[:, 0:1], scale=1.0),
                         ["gss0", "epsc"], ["rs0"])
                    P.op("act", lambda e: e.activation(rs[:, :], rs[:, :], AF.Exp, scale=-0.5), ["rs0"], ["rs0"])
                    for c in range(nch):
                        gc = self.acl[:, i, g0 + c:g0 + c + 1]
                        P.op("dve", lambda e, c=c, gc=gc: e.scalar_tensor_tensor(cn[:, c, :], craw[:, c, :], gc, rs[:, :], ALU.mult, ALU.mult),
                             ["craw", "rs0", "acl"], ["cn"])
                return [s0], True

            def run_jobs(jobs):
                n = len(jobs)
                emitted = [0] * n
                for t in range(n + 4):
                    if t < n and jobs[t][1]:
                        for j in range(t):
                            while emitted[j] < len(jobs[j][0]):
                                jobs[j][0][emitted[j]]()
                                emitted[j] += 1
                    if t < n:
                        jobs[t][0][0]()
                        emitted[t] = 1
                    for s_ in (1, 2, 3):
                        j = t - s_
                        if 0 <= j < n and emitted[j] == s_ and s_ < len(jobs[j][0]):
                            jobs[j][0][s_]()
                            emitted[j] += 1
                for j in range(n):
                    assert emitted[j] == len(jobs[j][0]), (j, emitted[j])

            load(0)
            for it, b in enumerate(blocks):
                xt = xts[it % 2]
                xk = "xt%d" % (it % 2)
                rtb = rtbs[it % 2]
                rk = "rtb%d" % (it % 2)
                if it + 1 < len(blocks):
                    load(it + 1)
                s = 1 if b == 0 else 0
                lat = b != 0
                cols = slice(b * TB, (b + 1) * TB)
                self.norm_mod(tl, xt[:, :, :], xk, TB, l, s, 1, ht, "ht")
                jobs = []
                ra_d = (6, rtb, rk, 0, 1, True) if lat else None
                ra_m = (7, rtb, rk, 2, 3, True) if lat else None
                for c in range(8):
                    dstT = self.dqT if c < 4 else self.dkT
                    gcol = self.acl[:, i, (0 if c < 4 else 1):(1 if c < 4 else 2)]
                    jobs.append(gn_job(win, "win", 8, c * 128, 128, ht, "ht", 4, gcol, ra_d, [(dstT[c % 4][:, cols], 0, 128)]))
                for tt in range(2):
                    vs, vsk = vst[tt], "vst%d" % tt
                    jobs.append(vt_job(ht, "ht", 8, tt, win, "win", 1024, vs, vsk, vs[:, :, :],
                                       self.dV[:, :, 2 * b + tt, :].rearrange("h p d -> p h d"), vs[:, :, :], 128))
                for c in range(3):
                    jobs.append(craw_job(1536, c))
                jobs.append(combine_job(3, 1, 6))
                for n_ in range(4):
                    jobs.append(gn_job(wqb, "wqb", 3, n_ * 128, 128, cn, "cn", 4, self.acl[:, i, 2:3], None,
                                       [(self.mqT[2 * n_ + hh][0:64, cols], hh * 64, (hh + 1) * 64) for hh in range(2)]))
                for r2 in range(2):
                    jobs.append(gn_job(wqb, "wqb", 3, 512 + r2 * 128, 128, cn, "cn", 5, self.acl[:, i, 4:5], ra_m,
                                       [(self.mqT[4 * r2 + j][64:96, cols], j * 32, (j + 1) * 32) for j in range(4)]))
                for c in range(2):
                    jobs.append(craw_job(1920, c))
                jobs.append(combine_job(2, 2, 9))
                for n_ in range(4):
                    jobs.append(gn_job(wkvb, "wkvb", 2, n_ * 128, 128, cn, "cn", 4, self.acl[:, i, 3:4], None,
                                       [(self.mknT[2 * n_ + hh][:, cols], hh * 64, (hh + 1) * 64) for hh in range(2)]))
                for tt in range(2):
                    ms, msk = mvst[tt], "mvst%d" % tt
                    jobs.append(vt_job(cn, "cn", 2, tt, wkvb, "wkvb", 512, ms, msk, ms[:, :, 0:64],
                                       self.mV[:, :, 2 * b + tt, :, :].rearrange("u p h e -> p u h e"),
                                       ms[:, :, :].rearrange("p (u h) e -> p u h e", h=2), 64))
                jobs.append(gn_job(win, "win", 8, 2176, 32, ht, "ht", 5, self.acl[0:32, i, 5:6], ra_m,
                                   [(self.krT[:, cols], 0, 32)]))
                run_jobs(jobs)

    def attn_pass(self, l, with_ctx):
        P = self.P
        i = l // 2
        T, NKT, NLB = self.T, self.NKT, self.NLB
        QB = 512
        with ExitStack() as st:
            kU = [self.sb(st, "kU%d" % k, [128, 2, T], BF16) for k in range(2)]
            vU = [self.sb(st, "vU%d" % k, [128, NKT, 130], BF16) for k in range(2)]
            qU = [self.sb(st, "qU%d" % k, [128, 2, QB], BF16) for k in range(2)]
            pT = [self.sb(st, "pT%d" % k, [128, 2, QB], BF16) for k in range(3)]
            sp_ = [self.ps(st, "sps%d" % k, [128, 2, 512]) for k in range(3)]
            O = [self.ps(st, "o%d" % k) for k in range(2)]
            L = [sp_[2][:, 0, :], sp_[2][:, 1, :]]
            rg = self.sb(st, "rg", [128, QB])
            og = [self.sb(st, "og%d" % k, [128, QB]) for k in range(2)]
            dd = self.sb(st, "dd", [128, QB])
            sqd = self.sb(st, "sqd", [128, QB], BF16)
            rsd = self.sb(st, "rsd", [128, QB])
            rbs = self.sb(st, "rbs", [128, QB])
            stg = [self.sb(st, "stg%d" % k, [128, QB], BF16) for k in range(4)]
            rrow = self.sb(st, "rrow", [128, QB])
            accs = [self.sb(st, "acc%d" % k, [128, 2, QB]) for k in range(2)]
            SPL = 768
            sel = self.sb(st, "sel", [128, 64])
            P.op("pool", lambda e: e.memset(rrow[:, :], 0.0), [], ["rrow"])
            P.op("pool", lambda e: e.memset(sel[:, :], 0.0), [], ["sel"])
            P.op("pool", lambda e: e.memset(sel[64:65, :], 1.0), [], ["sel"])
            qbs = []
            if with_ctx:
                qbs.append((0, TB, 2))
            for q in range(NLB // 2):
                qbs.append((TB + q * QB, QB, NKT))
            jobs = [(u, qb) for u in range(8) for qb in range(len(qbs))]

            def load_unit(u):
                k_, v_ = kU[u % 2], vU[u % 2]
                kk, vk = "kU%d" % (u % 2), "vU%d" % (u % 2)
                if u < 4:
                    self.dma("sp", k_[:, 0, :], self.dkT[u], [], [kk], kk)
                    self.dma("sp", v_[:, :, 0:128], self.dV[u], [], [vk], vk)
                else:
                    for h in range(2):
                        self.dma("sp", k_[0:64, h, :], self.mknT[2 * (u - 4) + h], [], [kk], kk)
                        self.dma("sp", k_[64:96, h, :], self.krT, [], [kk], kk)
                    self.dma("sp", v_[:, :, :], self.mV[u - 4].rearrange("p k h e -> p k (h e)"), [], [vk], vk)

            def load_q(j):
                u, qb = jobs[j]
                c0, nq, _ = qbs[qb]
                q_, qk = qU[j % 2], "qU%d" % (j % 2)
                if u < 4:
                    self.dma("sp", q_[:, 0, :nq], self.dqT[u][:, c0:c0 + nq], [], [qk], qk)
                else:
                    for h in range(2):
                        self.dma("sp", q_[0:96, h, :nq], self.mqT[2 * (u - 4) + h][:, c0:c0 + nq], [], [qk], qk)

            load_unit(0)
            load_q(0)
            scale_d = 1.0 / math.sqrt(64.0)
            scale_m = 1.0 / math.sqrt(96.0)
            it = 0
            nst = 0
            for j, (u, qb) in enumerate(jobs):
                c0, nq, nkt = qbs[qb]
                if qb == 0 and u + 1 < 8:
                    load_unit(u + 1)
                if j + 1 < len(jobs):
                    load_q(j + 1)
                k_, v_ = kU[u % 2], vU[u % 2]
                kk, vk = "kU%d" % (u % 2), "vU%d" % (u % 2)
                q_, qk = qU[j % 2], "qU%d" % (j % 2)
                diff = u < 4
                sc = scale_d if diff else scale_m

                def S_exp(kt, it0=it, k_=k_, q_=q_, kk=kk, qk=qk, nq=nq, diff=diff, sc=sc):
                    s_, sk = sp_[(it0 + kt) % 3], "sps%d" % ((it0 + kt) % 3)
                    p_, pk = pT[(it0 + kt) % 3], "pT%d" % ((it0 + kt) % 3)
                    kc = slice(kt * 128, (kt + 1) * 128)
                    for g in range(2):
                        if diff:
                            self.mm(s_[:, g, :nq], k_[g * 64:(g + 1) * 64, 0, kc], q_[g * 64:(g + 1) * 64, 0, :nq], True, True, [kk, qk], [sk])
                        else:
                            self.mm(s_[:, g, :nq], k_[0:96, g, kc], q_[0:96, g, :nq], True, True, [kk, qk], [sk])
                    P.op("act", lambda e: e.activation(p_[:, :, :nq], s_[:, :, :nq], AF.Exp, scale=sc), [sk], [pk])

                def PV(kt, it0=it, v_=v_, vk=vk, nq=nq, diff=diff, nkt=nkt):
                    p_, pk = pT[(it0 + kt) % 3], "pT%d" % ((it0 + kt) % 3)
                    for g in range(2):
                        if diff:
                            self.mm(O[g][:, :nq], v_[:, kt, 0:128], p_[:, g, :nq], kt == 0, kt == nkt - 1, [vk, pk], ["o%d" % g])
                        else:
                            self.mm(O[g][0:65, :nq], v_[:, kt, g * 65:(g + 1) * 65], p_[:, g, :nq], kt == 0, kt == nkt - 1, [vk, pk], ["o%d" % g])
                    if diff:
                        a_ = accs[kt % 2]
                        ka, kb = "accA%d" % (kt % 2), "accB%d" % (kt % 2)
                        parts = (("dve", a_[:, :, :nq], p_[:, :, :nq], ka),)
                        for (en, ao, po, kx) in parts:
                            if kt < 2:
                                P.op(en, lambda e, ao=ao, po=po: e.tensor_copy(ao, po), [pk], [kx])
                            else:
                                P.op(en, lambda e, ao=ao, po=po: e.tensor_tensor(ao, ao, po, ALU.add), [pk, kx], [kx])

                S_exp(0)
                if nkt > 1:
                    S_exp(1)
                for kt in range(nkt):
                    if kt + 2 < nkt:
                        S_exp(kt + 2)
                    PV(kt)
                it += nkt
                if diff:
                    for g in range(2):
                        for a2 in range(2):
                            self.mm(L[g][:, :nq], self.onesf[:, :], accs[a2][:, g, :nq], a2 == 0, a2 == 1,
                                    ["onesf", "accA%d" % a2, "accB%d" % a2], ["sps2"])
                    for g in range(2):
                        P.op("act", lambda e, g=g, nq=nq: e.activation(rg[:, :nq], L[g][:, :nq], AF.Ln), ["sps2"], ["rg"])
                        P.op("act", lambda e, nq=nq: e.activation(rg[:, :nq], rg[:, :nq], AF.Exp, scale=-1.0), ["rg"], ["rg"])
                        P.op("dve", lambda e, g=g, nq=nq: e.tensor_tensor(og[g][:, :nq], O[g][:, :nq], rg[:, :nq], ALU.mult),
                             ["o%d" % g, "rg"], ["og%d" % g])
                    P.op("dve", lambda e, nq=nq: e.scalar_tensor_tensor(dd[:, :nq], og[1][:, :nq], self.nlam[:, i:i + 1], og[0][:, :nq],
                                                                       ALU.mult, ALU.add), ["og0", "og1", "nlam"], ["dd"])
                    P.op("act", lambda e, nq=nq: e.activation(sqd[:, :nq], dd[:, :nq], AF.Square), ["dd"], ["sqd"])
                    self.mm(L[0][:, :nq], self.cm[:, 3, :], sqd[:, :nq], True, True, ["cm", "sqd"], ["sps2"])
                    P.op("act", lambda e, nq=nq: e.activation(rsd[:, :nq], L[0][:, :nq], AF.Ln, bias=self.epsc[:, 0:1], scale=1.0),
                         ["sps2", "epsc"], ["rsd"])
                    P.op("act", lambda e, nq=nq: e.activation(rsd[:, :nq], rsd[:, :nq], AF.Exp, scale=-0.5), ["rsd"], ["rsd"])
                    sg_, sgk = stg[nst % 4], "stg%d" % (nst % 4)
                    nst += 1
                    P.op("dve", lambda e, nq=nq, sg_=sg_: e.scalar_tensor_tensor(sg_[:, :nq], dd[:, :nq], self.acl[:, i, 12:13], rsd[:, :nq],
                                                                                ALU.mult, ALU.mult), ["dd", "rsd", "acl"], [sgk])
                    self.dma("sp", self.mT[u][:, c0:c0 + nq], sg_[:, :nq], [sgk], [], sgk)
                else:
                    for h in range(2):
                        P.op("act", lambda e, h=h, nq=nq: e.activation(rrow[64:65, :nq], O[h][64:65, :nq], AF.Ln), ["o%d" % h], ["rrow"])
                        P.op("act", lambda e, nq=nq: e.activation(rrow[64:65, :nq], rrow[64:65, :nq], AF.Exp, scale=-1.0), ["rrow"], ["rrow"])
                        self.mm(L[h][0:64, :nq], sel[:, :], rrow[:, :nq], True, True, ["sel", "rrow"], ["sps2"])
                        P.op("act", lambda e, h=h, nq=nq: e.copy(rbs[0:64, :nq], L[h][0:64, :nq]), ["sps2"], ["rbs"])
                        sg_, sgk = stg[nst % 4], "stg%d" % (nst % 4)
                        nst += 1
                        P.op("dve", lambda e, h=h, nq=nq, sg_=sg_: e.tensor_tensor(sg_[0:64, :nq], O[h][0:64, :nq], rbs[0:64, :nq], ALU.mult),
                             ["o%d" % h, "rbs"], [sgk])
                        self.dma("sp", self.mT[u][h * 64:(h + 1) * 64, c0:c0 + nq], sg_[0:64, :nq], [sgk], [], sgk)

    def merge_pass(self, l, blocks, src, dst, prefetch=None):
        P = self.P
        i = l // 2
        with ExitStack() as st:
            wo = self.sb(st, "wo", [128, 8, D], BF16)
            s_o = self.w_out[i].rearrange("(k p) n -> p k n", p=128)
            for k in range(8):
                self.dma("pool", wo[:, k, :], s_o[:, k, :], [], ["wo"], "wo")
            if prefetch is not None:
                prefetch()
            xts = [self.sb(st, "xt%d" % k, [128, 8, TB]) for k in range(3)]
            mts = [self.sb(st, "mt%d" % k, [128, 8, TB], BF16) for k in range(3)]
            xos = [self.sb(st, "xo%d" % k, [128, 8, TB]) for k in range(2)]
            ys = [self.ps(st, "y%d" % k) for k in range(2)]

            def load(it):
                b = blocks[it]
                self.dma("sp", xts[it % 3][:, :, :], src[b], [], ["xt%d" % (it % 3)], "xt%d" % (it % 3))
                self.dma("sp", mts[it % 3][:, :, :], self.mT[:, :, b * TB:(b + 1) * TB].rearrange("c p t -> p c t"),
                         [], ["mt%d" % (it % 3)], "mt%d" % (it % 3))

            load(0)
            if len(blocks) > 1:
                load(1)
            for it, b in enumerate(blocks):
                if it + 2 < len(blocks):
                    load(it + 2)
                s = 1 if b == 0 else 0
                xt, xk = xts[it % 3], "xt%d" % (it % 3)
                mt, mk = mts[it % 3], "mt%d" % (it % 3)
                xo, xok = xos[it % 2], "xo%d" % (it % 2)
                for d in range(8):
                    yp, yk = ys[d % 2], "y%d" % (d % 2)
                    for c in range(8):
                        self.mm(yp[:, :TB], wo[:, c, d * 128:(d + 1) * 128], mt[:, c, :], c == 0, c == 7, ["wo", mk], [yk])
                    G = self.modc[:, l, s, 5, d:d + 1]
                    o = xo[:, d, :]
                    xi = xt[:, d, :]
                    P.op("dve", lambda e, o=o, yp=yp, G=G, xi=xi: e.scalar_tensor_tensor(o, yp[:, :TB], G, xi, ALU.mult, ALU.add),
                         [yk, "modc", xk], [xok])
                self.dma("act", dst[b], xo[:, :, :], [xok], [], xok)


def _blocked(X):
    T = X.shape[0]
    return np.ascontiguousarray(X.reshape(T // TB, TB, 8, 128).transpose(0, 3, 2, 1))


def _unblocked(Y):
    nb = Y.shape[0]
    return np.ascontiguousarray(Y.transpose(0, 3, 2, 1).reshape(nb * TB, D))


def _cols(v):
    v = np.asarray(v, np.float32)
    lead = v.shape[:-1]
    r = v.reshape(*lead, 8, 128)
    r = np.moveaxis(r, -1, 0)
    return np.ascontiguousarray(r)


def const_tables(NLB):
    NB = 1 + NLB
    T = NB * TB
    S = NLB * TB
    cm = np.zeros((128, 9, 128), np.float32)
    cm[:, 8, :] = 1.0
    cm[:, 0, :] = 1.0 / 1024
    cm[:, 1, :] = 1.0 / 384
    cm[:, 2, :] = 1.0 / 256
    cm[:, 3, :] = 1.0 / 128
    for g in range(2):
        cm[g * 64:(g + 1) * 64, 4, g * 64:(g + 1) * 64] = 1.0 / 64
    for g in range(4):
        cm[g * 32:(g + 1) * 32, 5, g * 32:(g + 1) * 32] = 1.0 / 32
    for m in range(128):
        d = m % 64
        half = (d % 32) // 16
        partner = m + 16 if half == 0 else m - 16
        cm[partner, 6, m] = 1.0
    for m in range(128):
        d = m % 32
        half = (d % 16) // 8
        partner = m + 8 if half == 0 else m - 8
        cm[partner, 7, m] = 1.0
    invc = np.zeros((NB, 4, TB), np.float32)
    for b in range(NB):
        L = CTX if b == 0 else S
        t = np.arange(TB) + (0 if b == 0 else (b - 1) * TB)
        for gi, w in enumerate((2, 4, 8, 16)):
            lo = np.clip(t - w // 2, 0, L)
            hi = np.clip(t + (w - w // 2), 0, L)
            invc[b, gi] = 1.0 / (hi - lo).astype(np.float32)
    rt = np.zeros((4, 128, T), np.float32)
    rt[0, :, :] = 1.0
    rt[2, :, :] = 1.0
    tt = np.arange(S)
    rows = (tt // GRID_W).astype(np.float32)
    cols = (tt % GRID_W).astype(np.float32)
    for ti, dim in ((0, 64), (2, 32)):
        q = dim // 4
        freqs = (10000.0 ** (-np.arange(q, dtype=np.float32) / q)).astype(np.float32)
        for p in range(128):
            d = p % dim
            axis = d // (dim // 2)
            half = (d % (dim // 2)) // q
            f = d % q
            pos = rows if axis == 0 else cols
            ang = (pos * freqs[f]).astype(np.float32)
            rt[ti, p, CTX:] = np.cos(ang)
            rt[ti + 1, p, CTX:] = np.sin(ang) * (-1.0 if half == 0 else 1.0)
    return cm, invc, rt


def host_inputs(inp, b, NLB, consts):
    cm, invc, rt = consts
    S = NLB * TB
    f32 = np.float32
    X = np.concatenate([np.asarray(inp["ctx"][b], f32), np.asarray(inp["x"][b][:S], f32)], axis=0)
    m = {}
    m["xin"] = _blocked(X)
    cc = np.stack([np.asarray(inp["c"][b], f32), np.asarray(inp["c_ctx"], f32)], axis=0)
    m["cT"] = np.ascontiguousarray(cc.reshape(2, 8, 128).transpose(2, 1, 0))
    m["mod_w"] = np.asarray(inp["mod_w"], f32)
    m["mod_bT"] = np.ascontiguousarray(np.asarray(inp["mod_b"], f32).reshape(4, 72, 128).transpose(2, 0, 1))
    m["gT"] = np.ascontiguousarray(np.asarray(inp["norm_g"], f32).reshape(4, 3, 8, 128).transpose(3, 0, 1, 2))
    m["pscl"] = np.ascontiguousarray(np.asarray(inp["pool_scale"], f32).reshape(2, 8, 128).transpose(2, 0, 1))
    m["ffn_w_gate"] = np.asarray(inp["ffn_w_gate"], f32)
    m["ffn_w_up"] = np.asarray(inp["ffn_w_up"], f32)
    m["ffn_w_down"] = np.asarray(inp["ffn_w_down"], f32)
    m["pool_w"] = np.asarray(inp["pool_w"], f32)
    m["invc"] = invc
    m["attn_w_in"] = np.asarray(inp["attn_w_in"], f32)
    wq = np.asarray(inp["mla_w_q_b"], f32).reshape(2, 384, 8, 96)
    m["w_qb"] = np.ascontiguousarray(np.concatenate([wq[..., :64].reshape(2, 384, 512), wq[..., 64:].reshape(2, 384, 256)], axis=-1))
    wkv = np.asarray(inp["mla_w_kv_b"], f32).reshape(2, 256, 8, 128)
    m["w_kvb"] = np.ascontiguousarray(np.concatenate([wkv[..., :64].reshape(2, 256, 512), wkv[..., 64:].reshape(2, 256, 512)], axis=-1))
    m["attn_w_out"] = np.asarray(inp["attn_w_out"], f32)
    ac = np.zeros((128, 2, 16), f32)
    for i in range(2):
        ac[:, i, 0] = np.tile(np.asarray(inp["diff_qk_g"][i, 0], f32), 2)
        ac[:, i, 1] = np.tile(np.asarray(inp["diff_qk_g"][i, 1], f32), 2)
        ac[:, i, 2] = np.tile(np.asarray(inp["mla_nope_g"][i, 0], f32), 2)
        ac[:, i, 3] = np.tile(np.asarray(inp["mla_nope_g"][i, 1], f32), 2)
        ac[:, i, 4] = np.tile(np.asarray(inp["mla_rope_g"][i, 0], f32), 4)
        ac[:, i, 5] = np.tile(np.asarray(inp["mla_rope_g"][i, 1], f32), 4)
        ac[:, i, 6:9] = np.asarray(inp["mla_q_a_g"][i], f32).reshape(3, 128).T
        ac[:, i, 9:11] = np.asarray(inp["mla_kv_a_g"][i], f32).reshape(2, 128).T
        ac[:, i, 11] = np.asarray(inp["diff_subln_g"][i], f32)
    m["acols"] = ac
    m["lamv"] = np.ascontiguousarray(np.asarray(inp["diff_lambda"], f32).reshape(2, 1, 256))
    m["ropet"] = rt
    m["cmat"] = cm
    return m


_CACHE = {}


def run(inp, NLB, layers, stop=None, cores=N_CORES):
    key = (NLB, tuple(layers), stop)
    if key not in _CACHE:
        _CACHE[key] = Builder(NLB, layers, stop).build()
    nc = _CACHE[key]
    consts = const_tables(NLB)
    in_maps = [host_inputs(inp, b, NLB, consts) for b in range(cores)]
    res = run_bass_kernel_spmd(nc, in_maps, core_ids=list(range(cores)))
    outs = [_unblocked(np.asarray(r["out"])) for r in res.results]
    return np.stack(outs, axis=0)


def kernel(**inputs):
    return run(inputs, 32, (0, 1, 2, 3)).astype(np.float32)
```
